# Optimizing a Trainium2 kernel written in Bass

```python
import math, functools
import jax, jax.numpy as jnp
from jax import lax
import numpy as np

D_MODEL = 1024
BATCH = 32
SEQ = 256
DEPTH = 2
DEC_BATCH = 2
DEC_SEQ = 2048
PAST_LEN = 512

GRID_W = 64
N_MIXERS = 2
N_HEADS = 16
HEAD_DIM = D_MODEL // N_HEADS
N_FOURIER_GROUPS = 4
FOURIER_GROUP = D_MODEL // N_FOURIER_GROUPS
WIN_ROWS_MAX = 8
WIN_COLS = 16
D_FF = -(-8 * D_MODEL // (3 * 256)) * 256
N_FOURIER_LAYERS = (DEPTH + N_MIXERS - 1) // N_MIXERS
N_NA_LAYERS = DEPTH // N_MIXERS
N_NORMS = 4
EPS = 1e-6
Q_BLOCK = 128
NEG_INF = -1e30

kernel_name = "hybrid_fourier_natten_prefix_dit_step"


def _rmsnorm(x, g):
    xf = x.astype(jnp.float32)
    y = xf * lax.rsqrt(jnp.mean(xf * xf, axis=-1, keepdims=True) + EPS)
    return (y * g.astype(jnp.float32)).astype(x.dtype)


def _modulation(cond, w, b):
    m = jax.nn.silu(cond) @ w + b
    return jnp.split(m[:, None, :], 6, axis=-1)


def _fourier_mix(h, w_out):
    b, s, d = h.shape
    hg = h.astype(jnp.float32).reshape(b, s, N_FOURIER_GROUPS, FOURIER_GROUP)
    f = jnp.fft.fft2(hg, axes=(1, 3)).real
    return f.reshape(b, s, d).astype(h.dtype) @ w_out


def _qkv(h, w_qkv):
    b, s, _ = h.shape
    q, k, v = jnp.split(h @ w_qkv, 3, axis=-1)
    sh = (b, s, N_HEADS, HEAD_DIM)
    return q.reshape(sh), k.reshape(sh), v.reshape(sh)


def _dense_attn(q, k, v):
    b, s, h, dh = q.shape
    scale = dh ** -0.5
    qb = q.reshape(b, s // Q_BLOCK, Q_BLOCK, h, dh).transpose(1, 0, 2, 3, 4)

    def one_block(qi):
        sc = jnp.einsum('bqhd,bkhd->bhqk', qi, k).astype(jnp.float32) * scale
        p = jax.nn.softmax(sc, axis=-1)
        return jnp.einsum('bhqk,bkhd->bqhd', p.astype(v.dtype), v)

    out = lax.map(one_block, qb)
    return out.transpose(1, 0, 2, 3, 4).reshape(b, s, h, dh)


def _na_latent(q, k, v, ck, cv, rpb):
    b, s, h, dh = q.shape
    rows = s // GRID_W
    kr = min(WIN_ROWS_MAX, rows)
    scale = dh ** -0.5
    qg = q.reshape(b, rows, GRID_W, h, dh)
    kg = k.reshape(b, rows, GRID_W, h, dh)
    vg = v.reshape(b, rows, GRID_W, h, dh)
    r = jnp.arange(rows)
    rs = jnp.clip(r - kr // 2, 0, rows - kr)
    row_idx = rs[:, None] + jnp.arange(kr)[None, :]
    kb = kg[:, row_idx]
    vb = vg[:, row_idx]
    col = jnp.arange(GRID_W)
    cs = jnp.clip(col - WIN_COLS // 2, 0, GRID_W - WIN_COLS)
    col_mask = (col[None, :] >= cs[:, None]) & (col[None, :] < cs[:, None] + WIN_COLS)
    dr = row_idx - r[:, None]
    dc = jnp.clip(col[None, :] - col[:, None], -(WIN_COLS - 1), WIN_COLS - 1)
    bias = rpb[:, dr[:, None, :, None] + (WIN_ROWS_MAX - 1), dc[None, :, None, :] + (WIN_COLS - 1)]
    bias = jnp.where(col_mask[None, None, :, None, :], bias.astype(jnp.float32), NEG_INF)
    s_loc = jnp.einsum('brqhd,brikhd->bhrqik', qg, kb).astype(jnp.float32) * scale + bias[None]
    s_loc = s_loc.reshape(b, h, rows, GRID_W, kr * GRID_W)
    s_ctx = jnp.einsum('brqhd,bnhd->bhrqn', qg, ck).astype(jnp.float32) * scale
    p = jax.nn.softmax(jnp.concatenate([s_loc, s_ctx], axis=-1), axis=-1)
    p_loc = p[..., :kr * GRID_W].reshape(b, h, rows, GRID_W, kr, GRID_W).astype(v.dtype)
    p_ctx = p[..., kr * GRID_W:].astype(v.dtype)
    out = (jnp.einsum('bhrqik,brikhd->brqhd', p_loc, vb)
           + jnp.einsum('bhrqn,bnhd->brqhd', p_ctx, cv))
    return out.reshape(b, s, h, dh)


def _swiglu(h, w_gate, w_up, w_down):
    return (jax.nn.silu(h @ w_gate) * (h @ w_up)) @ w_down


def setup_inputs(seed: int = 0) -> dict:
    key = jax.random.key(seed)
    ks = jax.random.split(key, 16)
    f32 = jnp.float32
    n = lambda k, shape, s: (jax.random.normal(k, shape, f32) * s)
    return {
        "x_prompt": n(ks[0], (BATCH, SEQ, D_MODEL), 1.0),
        "x_sample": n(ks[1], (DEC_BATCH, DEC_SEQ, D_MODEL), 1.0),
        "c": n(ks[2], (DEC_BATCH, D_MODEL), 1.0),
        "cache_k": n(ks[3], (DEC_BATCH, N_NA_LAYERS, PAST_LEN, N_HEADS, HEAD_DIM), 1.0),
        "cache_v": n(ks[4], (DEC_BATCH, N_NA_LAYERS, PAST_LEN, N_HEADS, HEAD_DIM), 1.0),
        "c_ctx": n(ks[5], (D_MODEL,), 1.0),
        "ada_w": n(ks[6], (DEPTH, D_MODEL, 6 * D_MODEL), 0.5 * D_MODEL ** -0.5),
        "ada_b": n(ks[7], (DEPTH, 6 * D_MODEL), 0.02),
        "norm_g": 1.0 + n(ks[8], (DEPTH, N_NORMS, D_MODEL), 0.01),
        "fourier_w_out": n(ks[9], (N_FOURIER_LAYERS, D_MODEL, D_MODEL), D_MODEL ** -0.5),
        "na_w_qkv": n(ks[10], (N_NA_LAYERS, D_MODEL, 3 * D_MODEL), D_MODEL ** -0.5),
        "na_rpb": n(ks[11], (N_NA_LAYERS, N_HEADS, 2 * WIN_ROWS_MAX - 1, 2 * WIN_COLS - 1), 0.1),
        "na_w_out": n(ks[12], (N_NA_LAYERS, D_MODEL, D_MODEL), D_MODEL ** -0.5),
        "ffn_w_gate": n(ks[13], (DEPTH, D_MODEL, D_FF), D_MODEL ** -0.5),
        "ffn_w_up": n(ks[14], (DEPTH, D_MODEL, D_FF), D_MODEL ** -0.5),
        "ffn_w_down": n(ks[15], (DEPTH, D_FF, D_MODEL), D_FF ** -0.5),
    }


def reference(x_prompt, x_sample, c, cache_k, cache_v, c_ctx, ada_w, ada_b, norm_g,
              fourier_w_out, na_w_qkv, na_rpb, na_w_out, ffn_w_gate, ffn_w_up, ffn_w_down):
    xp, xs = x_prompt, x_sample
    new_k, new_v = [], []
    for i in range(DEPTH):
        g = norm_g[i]
        sh_mp, sc_mp, gt_mp, sh_fp, sc_fp, gt_fp = _modulation(c_ctx[None, :], ada_w[i], ada_b[i])
        sh_ms, sc_ms, gt_ms, sh_fs, sc_fs, gt_fs = _modulation(c, ada_w[i], ada_b[i])
        hp = _rmsnorm(xp, g[0]) * (1.0 + sc_mp) + sh_mp
        hs = _rmsnorm(xs, g[0]) * (1.0 + sc_ms) + sh_ms
        j = i // N_MIXERS
        if i % N_MIXERS == 0:
            mp = _fourier_mix(hp, fourier_w_out[j])
            ms = _fourier_mix(hs, fourier_w_out[j])
        else:
            qp, kp, vp = _qkv(hp, na_w_qkv[j])
            new_k.append(kp)
            new_v.append(vp)
            op = _dense_attn(qp, kp, vp)
            mp = op.reshape(xp.shape) @ na_w_out[j]
            qs, ks_, vs = _qkv(hs, na_w_qkv[j])
            os_ = _na_latent(qs, ks_, vs, cache_k[:, j], cache_v[:, j], na_rpb[j])
            ms = os_.reshape(xs.shape) @ na_w_out[j]
        xp = xp + gt_mp * _rmsnorm(mp, g[1])
        xs = xs + gt_ms * _rmsnorm(ms, g[1])
        fp = _rmsnorm(xp, g[2]) * (1.0 + sc_fp) + sh_fp
        fs = _rmsnorm(xs, g[2]) * (1.0 + sc_fs) + sh_fs
        xp = xp + gt_fp * _rmsnorm(_swiglu(fp, ffn_w_gate[i], ffn_w_up[i], ffn_w_down[i]), g[3])
        xs = xs + gt_fs * _rmsnorm(_swiglu(fs, ffn_w_gate[i], ffn_w_up[i], ffn_w_down[i]), g[3])
    new_cache_k = jnp.stack(new_k, axis=1)
    new_cache_v = jnp.stack(new_v, axis=1)
    return (xp, xs, new_cache_k, new_cache_v)
```

```python
import numpy as np
import ml_dtypes
from contextlib import ExitStack
import concourse.bass as bass
import concourse.mybir as mybir
from concourse.bass_utils import run_bass_kernel_spmd

F32 = mybir.dt.float32
BF16 = mybir.dt.bfloat16
AF = mybir.ActivationFunctionType
ALU = mybir.AluOpType

D = 1024
DFF = 2816
NFC = 22
NH = 16
EPS = 1e-6
NEG = -30000.0
PG = 1024
ARENA_BYTES = 207 * 1024
ENGS = ("pe", "act", "dve", "pool", "sp")
WROW = [4, 6, 8, 10, 0, 2, 12, 14]


class Op:
    __slots__ = ("eng", "fn", "deps", "dma", "marked", "ev", "reuse_wait")

    def __init__(self, eng, fn, dma):
        self.eng = eng
        self.fn = fn
        self.dma = dma
        self.deps = []
        self.marked = False
        self.ev = None
        self.reuse_wait = None


class Sched:
    def __init__(self, nc, stack, n_dma_sems=40):
        self.nc = nc
        self.ops = {e: [] for e in ENGS}
        self.last_writer = {}
        self.readers = {}
        self.esem = {e: stack.enter_context(nc.semaphore("s_" + e)) for e in ENGS}
        self.dsems = [stack.enter_context(nc.semaphore("d%d" % i)) for i in range(n_dma_sems)]
        self.duse = [0] * n_dma_sems
        self.dlast = [None] * n_dma_sems
        half = n_dma_sems * 3 // 5
        self.dpool = {"pool": list(range(0, half)), "sp": list(range(half, n_dma_sems))}
        self.dnext = {"pool": 0, "sp": 0}

    @staticmethod
    def _keys(items):
        out = []
        for k in items:
            if k is None:
                continue
            if isinstance(k, V):
                out.extend(k.keys())
            elif isinstance(k, (list, tuple)) and k and isinstance(k[0], V):
                for kk in k:
                    out.extend(kk.keys())
            else:
                out.append(k)
        return out

    def add(self, eng, fn, reads=(), writes=(), dma=False):
        op = Op(eng, fn, dma)
        reads = self._keys(reads)
        writes = self._keys(writes)
        deps = {}
        for k in reads:
            w = self.last_writer.get(k)
            if w is not None:
                deps[id(w)] = (w, True)
            if isinstance(k, str) and k.startswith("ps"):
                for r in self.readers.get(k, ()):
                    if r.eng != eng and id(r) not in deps:
                        deps[id(r)] = (r, False)
        for k in writes:
            w = self.last_writer.get(k)
            if w is not None and id(w) not in deps:
                deps[id(w)] = (w, False)
            for r in self.readers.get(k, ()):
                if id(r) not in deps:
                    deps[id(r)] = (r, False)
        for d, raw in deps.values():
            if d is op:
                continue
            if d.eng == eng and not d.dma and not dma:
                if eng == "pe":
                    continue
            op.deps.append(d)
            if not d.dma:
                d.marked = True
        for k in writes:
            self.last_writer[k] = op
            self.readers[k] = []
        for k in reads:
            self.readers.setdefault(k, []).append(op)
        if dma:
            pool = self.dpool[eng]
            i = pool[self.dnext[eng]]
            self.dnext[eng] = (self.dnext[eng] + 1) % len(pool)
            if self.dlast[i] is not None:
                op.reuse_wait = self.dlast[i]
            self.duse[i] += 1
            op.ev = (self.dsems[i], 16 * self.duse[i])
            self.dlast[i] = op
        self.ops[eng].append(op)
        return op

    def emit_engine(self, e, handle):
        seen = {}
        for op in self.ops[e]:
            waits = [d.ev for d in op.deps]
            if op.reuse_wait is not None:
                waits.append(op.reuse_wait.ev)
            for sem, val in waits:
                key = id(sem)
                if seen.get(key, 0) >= val:
                    continue
                seen[key] = val
                handle.wait_ge(sem, val)
            if op.fn is None:
                continue
            ins = op.fn(handle)
            if op.dma:
                ins.then_inc(op.ev[0], 16)
            elif op.marked:
                ins.then_inc(op.ev[0], 1)

    def emit(self):
        for e in ENGS:
            c = 0
            for op in self.ops[e]:
                if not op.dma and op.marked:
                    c += 1
                    op.ev = (self.esem[e], c)
        with self.nc.Block() as block:
            @block.tensor
            def _(h):
                self.emit_engine("pe", h)

            @block.scalar
            def _(h):
                self.emit_engine("act", h)

            @block.vector
            def _(h):
                self.emit_engine("dve", h)

            @block.gpsimd
            def _(h):
                self.emit_engine("pool", h)

            @block.sync
            def _(h):
                self.emit_engine("sp", h)


class V:
    def __init__(self, ap, off, esz, shape, strides, keyname=None):
        self.ap = ap
        self.off = off
        self.esz = esz
        self.shape = tuple(shape)
        self.strides = tuple(strides)
        self.keyname = keyname

    def keys(self):
        if self.keyname is not None:
            return [self.keyname]
        span = sum((n - 1) * s for n, s in zip(self.shape, self.strides)) + 1
        lo = self.off
        hi = self.off + span * self.esz
        return [("A", p) for p in range(lo // PG, (hi - 1) // PG + 1)]

    def __getitem__(self, idx):
        if not isinstance(idx, tuple):
            idx = (idx,)
        idx = idx + (slice(None),) * (len(self.shape) - len(idx))
        off = self.off
        shape, strides = [], []
        for i, n, s in zip(idx, self.shape, self.strides):
            if isinstance(i, slice):
                a, b, st = i.indices(n)
                cnt = len(range(a, b, st))
                off += a * s * self.esz
                shape.append(cnt)
                strides.append(s * st)
            else:
                off += i * s * self.esz
        ap = self.ap[(slice(None),) + idx]
        return V(ap, off, self.esz, shape, strides, self.keyname)

    def pp(self, lo, hi):
        return V(self.ap[lo:hi], self.off, self.esz, self.shape, self.strides, self.keyname)


class Arena:
    def __init__(self, ap_bf16, nbytes):
        self.ap = ap_bf16
        self.cap = nbytes
        self.top = 0

    def alloc(self, shape, dt, page=False):
        if isinstance(shape, int):
            shape = (shape,)
        esz = 4 if dt == F32 else 2
        n = int(np.prod(shape)) * esz
        al = PG if (n >= PG or page) else 64
        off = (self.top + al - 1) // al * al
        assert off + n <= self.cap, ("arena overflow", off, n, self.cap)
        self.top = off + n
        ap = self.ap[:, off // 2:(off + n) // 2]
        if dt == F32:
            ap = ap.bitcast(F32)
        if len(shape) == 2:
            ap = ap.rearrange("p (a b) -> p a b", a=shape[0])
        elif len(shape) == 3:
            ap = ap.rearrange("p (a b c) -> p a b c", a=shape[0], b=shape[1])
        strides = [int(np.prod(shape[i + 1:])) for i in range(len(shape))]
        return V(ap, off, esz, shape, strides)

    def mark(self):
        return self.top

    def release(self, m):
        self.top = m


class Ring:
    def __init__(self, arena, n, shape, dt):
        self.slots = [arena.alloc(shape, dt) for _ in range(n)]
        self.i = 0

    def next(self):
        s = self.slots[self.i]
        self.i = (self.i + 1) % len(self.slots)
        return s


def build_program(debug=False, stage_limit=None):
    nc = bass.Bass("TRN2", target_bir_lowering=False)

    def din(name, shape):
        return nc.dram_tensor(name, list(shape), F32, kind="ExternalInput").ap()

    def dout(name, shape):
        return nc.dram_tensor(name, list(shape), F32, kind="ExternalOutput").ap()

    xT_in = {"p": din("xT_p", (128, 8192)), "s": din("xT_s", (128, 8192))}
    xtok_in = {"p": din("xtok_p", (128, 8192)), "s": din("xtok_s", (128, 16384))}
    smalls_in = din("smalls", (128, 180))
    adaW = din("adaW", (48, 128, 2048))
    fwW = din("fwW", (4, 128, 2048))
    qkW = din("qkW", (8, 128, 2048))
    vW = din("vW", (4, 128, 2048))
    owW = din("owW", (4, 128, 2048))
    guW = din("guW", (44, 128, 2048))
    dW = din("dW", (16, 128, 2816))
    ckT_in = din("ckT", (128, 4096))
    cv_in = din("cv", (128, 4096))
    BM_in = din("BM", (16, 128, 4096))
    dft256_in = din("dft256", (128, 1536))
    dft2k_in = nc.dram_tensor("dft2k", [4, 128, 8192], BF16, kind="ExternalInput").ap()
    yT_out = {"p": dout("yT_p", (128, 8192)), "s": dout("yT_s", (128, 4096))}
    kT_out = dout("kT_p", (128, 8192))
    v_out = dout("v_p", (128, 8192))
    dbg_outs = {}

    with ExitStack() as st:
        S = Sched(nc, st)
        arena_t = st.enter_context(nc.sbuf_tensor("arena", [128, ARENA_BYTES // 2], BF16))
        A = Arena(arena_t[:, :], ARENA_BYTES)
        banks = []
        for i in range(8):
            pt = st.enter_context(nc.psum_tensor("ps%d" % i, [128, 512], F32))
            banks.append(V(pt[:, :], 0, 4, (512,), (1,), keyname="ps%d" % i))
        bstate = {"i": 0}

        def nb():
            b = banks[bstate["i"]]
            bstate["i"] = (bstate["i"] + 1) % 8
            return b

        def bf16bank(b):
            return V(b.ap.bitcast(BF16), 0, 2, (1024,), (1,), keyname=b.keyname)

        out_events = []

        def mm(out, lhsT, rhs, start, stop):
            S.add("pe", lambda e: e.matmul(out.ap, lhsT=lhsT.ap, rhs=rhs.ap, start=start, stop=stop),
                  reads=[lhsT, rhs], writes=[out])

        def transpose(out, in_, ident):
            S.add("pe", lambda e: e.transpose(out=out.ap, in_=in_.ap, identity=ident.ap),
                  reads=[in_, ident], writes=[out])

        def act(out, in_, func, bias=None, scale=None, accum_out=None):
            kw = {}
            rd = [in_]
            wr = [out]
            if bias is not None:
                kw["bias"] = bias.ap if isinstance(bias, V) else bias
                if isinstance(bias, V):
                    rd.append(bias)
            if scale is not None:
                kw["scale"] = scale.ap if isinstance(scale, V) else scale
                if isinstance(scale, V):
                    rd.append(scale)
            if accum_out is not None:
                kw["accum_out"] = accum_out.ap
                wr.append(accum_out)
            S.add("act", lambda e: e.activation(out=out.ap, in_=in_.ap, func=func, **kw), reads=rd, writes=wr)

        def tt(out, in0, in1, op, eng="dve"):
            S.add(eng, lambda e: e.tensor_tensor(out=out.ap, in0=in0.ap, in1=in1.ap, op=op),
                  reads=[in0, in1], writes=[out])

        def ts(out, in0, s1, op0, s2=None, op1=None, eng="dve"):
            rd = [in0]
            a1 = s1.ap if isinstance(s1, V) else s1
            a2 = s2.ap if isinstance(s2, V) else s2
            if isinstance(s1, V):
                rd.append(s1)
            if isinstance(s2, V):
                rd.append(s2)
            if op1 is None:
                S.add(eng, lambda e: e.tensor_scalar(out=out.ap, in0=in0.ap, scalar1=a1, scalar2=None, op0=op0),
                      reads=rd, writes=[out])
            else:
                S.add(eng, lambda e: e.tensor_scalar(out=out.ap, in0=in0.ap, scalar1=a1, scalar2=a2, op0=op0, op1=op1),
                      reads=rd, writes=[out])

        def stt(out, in0, scalar, in1, op0, op1, eng="dve"):
            rd = [in0, in1]
            sc = scalar.ap if isinstance(scalar, V) else scalar
            if isinstance(scalar, V):
                rd.append(scalar)
            S.add(eng, lambda e: e.scalar_tensor_tensor(out=out.ap, in0=in0.ap, scalar=sc, in1=in1.ap, op0=op0, op1=op1),
                  reads=rd, writes=[out])

        def copy(out, in_, eng="dve"):
            if eng == "act":
                act(out, in_, AF.Copy)
            else:
                S.add(eng, lambda e: e.tensor_copy(out=out.ap, in_=in_.ap), reads=[in_], writes=[out])

        def memset(out, val, eng="dve"):
            S.add(eng, lambda e: e.memset(out.ap, val), writes=[out])

        def recip(out, in_):
            S.add("dve", lambda e: e.reciprocal(out=out.ap, in_=in_.ap), reads=[in_], writes=[out])

        def dma_in(q, out, src, after=()):
            S.add(q, lambda e: e.dma_start(out=out.ap, in_=src), reads=list(after), writes=[out], dma=True)

        def dma_out(q, dst, in_):
            op = S.add(q, lambda e: e.dma_start(out=dst, in_=in_.ap), reads=[in_], dma=True)
            out_events.append(op)

        def dump(name, v, n):
            if not debug:
                return
            t = dout("dbg_" + name, (128, n))
            dbg_outs[name] = n
            if v.esz == 2:
                stg = A.alloc((n,), F32)
                copy(stg, v)
                dma_out("sp", t, stg)
            else:
                dma_out("sp", t, v)

        evac_rr = {"i": 0}

        def evac_copy(out, in_):
            evac_rr["i"] ^= 1
            copy(out, in_, eng="act" if evac_rr["i"] else "dve")

        ident = A.alloc((128,), BF16)
        ones = A.alloc((128,), BF16)
        dft256 = A.alloc((3, 2, 256), BF16)
        scT = A.alloc((8, 2), BF16)
        smalls = A.alloc((180,), F32, page=True)

        def sub(a, shape):
            n = int(np.prod(shape))
            ap = smalls.ap[:, a:a + n]
            if len(shape) == 2:
                ap = ap.rearrange("p (a b) -> p a b", a=shape[0])
            elif len(shape) == 3:
                ap = ap.rearrange("p (a b c) -> p a b c", a=shape[0], b=shape[1])
            strides = [int(np.prod(shape[i + 1:])) for i in range(len(shape))]
            return V(ap, smalls.off + 4 * a, 4, shape, strides)
        cT = sub(0, (8, 2))
        adaB = sub(16, (2, 48))
        gT = sub(112, (2, 4, 8))
        scal = sub(176, (4,))
        modT_pad = A.alloc((4,), F32, page=True)
        modT = A.alloc((2, 2, 48), F32)
        dv = A.alloc((2, 2, 4 * 8), F32)
        bdc = A.alloc((2, 8), F32)
        xT = A.alloc((8, 1024), F32)
        w8 = Ring(A, 5, (2, 8, 128), BF16)
        wd = Ring(A, 3, (NFC, 128), BF16)
        rstd_r = Ring(A, 2, (512,), F32)
        ln_r = Ring(A, 2, (512,), F32)
        tmp_r = Ring(A, 3, (512,), F32)
        eps_v = scal[2:3]

        memset(ident, 0.0, eng="pool")
        S.add("pool", lambda e: e.affine_select(out=ident.ap, in_=ident.ap, pattern=[[-1, 128]],
                                                compare_op=ALU.not_equal, fill=1.0, base=0, channel_multiplier=1),
              reads=[ident], writes=[ident])
        memset(ones, 1.0, eng="pool")
        dma_in("sp", smalls, smalls_in)
        dma_in("pool", dft256, dft256_in.rearrange("p (a b c) -> p a b c", a=3, b=2))
        act(scT, cT, AF.Silu)

        def dvv(l, cd, kind):
            return dv[l, cd, kind * 8:(kind + 1) * 8]

        bgq = []

        def bg_step(n=1):
            for _ in range(n):
                if bgq:
                    bgq.pop(0)()

        def bg_flush():
            while bgq:
                bgq.pop(0)()

        def mod_part(l, part, which="all", gate=()):
            def pair_job(j):
                slot = w8.next()
                dma_in("pool", slot, adaW[l * 24 + part * 12 + j].rearrange("p (a b c) -> p a b c", a=2, b=8), after=gate)
                bank = nb()
                for a in range(2):
                    for dc in range(8):
                        mm(bank[a * 2:a * 2 + 2], slot[a, dc], scT[dc], dc == 0, dc == 7)
                bv = V(bank.ap[:, 0:4].rearrange("p (o c) -> p o c", c=2), 0, 4, (2, 2), (2, 1), keyname=bank.keyname)
                o = part * 24 + 2 * j
                for cd in range(2):
                    tt(modT[l, cd, o:o + 2], bv[:, cd], adaB[l, o:o + 2], ALU.add)

            def fin_a():
                o = part * 24
                for cd in range(2):
                    stt(dvv(l, cd, 2 * part), modT[l, cd, o + 8:o + 16], 1.0, gT[l, 2 * part], ALU.add, ALU.mult)
                if l == 0 and part == 0:
                    ts(bdc[0], modT[0, 0, 0:8], scal[1:2], ALU.mult)
                    ts(bdc[1], modT[0, 1, 0:8], scal[0:1], ALU.mult)

            def fin_g():
                o = part * 24
                for cd in range(2):
                    tt(dvv(l, cd, 2 * part + 1), modT[l, cd, o + 16:o + 24], gT[l, 2 * part + 1], ALU.mult)
            if which in ("all", "ab"):
                for j in (4, 5, 6, 7, 0, 1, 2, 3):
                    bgq.append(lambda j=j: pair_job(j))
                bgq.append(fin_a)
            if which in ("all", "gate"):
                for j in (8, 9, 10, 11):
                    bgq.append(lambda j=j: pair_job(j))
                bgq.append(fin_g)

        def sqs(sq, blk):
            o = (blk * 512) % sq.shape[1]
            return slice(o, o + 512)

        def stats_to_rstd(sq, blk):
            bank = nb()
            for c in range(8):
                mm(bank, ones, sq[c, sqs(sq, blk)], c == 0, c == 7)
            ln = ln_r.next()
            rstd = rstd_r.next()
            act(ln, bank, AF.Ln, bias=eps_v, scale=1.0 / D)
            act(rstd, ln, AF.Exp, scale=-0.5)
            return rstd

        pending_tails = {}

        def flush_tails():
            for blk in sorted(pending_tails):
                pending_tails.pop(blk)()

        def set_tails(blks, msb, sq, gtg):
            for blk in blks:
                rstd = stats_to_rstd(sq, blk)
                pending_tails[blk] = (lambda blk=blk, rstd=rstd: postnorm_apply_with(blk, msb, rstd, gtg))

        def prenorm(blks, avec, bvec, hT, sq):
            for blk in blks:
                t = pending_tails.pop(blk, None)
                if t is not None:
                    t()
                cs = slice(blk * 512, (blk + 1) * 512)
                for c in range(8):
                    act(sq[c, sqs(sq, blk)], xT[c, cs], AF.Square)
                rstd = stats_to_rstd(sq, blk)
                for c in range(8):
                    tmp = tmp_r.next()
                    stt(tmp, xT[c, cs], avec[c:c + 1], rstd, ALU.mult, ALU.mult)
                    act(hT[c, cs], tmp, AF.Identity, bias=bvec[c:c + 1])

        def postnorm_evac(bank, oc, blk, gtg, msb, sq):
            cs = slice(blk * 512, (blk + 1) * 512)
            act(sq[oc, sqs(sq, blk)], bank, AF.Square)
            act(msb[oc, cs], bank, AF.Copy)

        def postnorm_apply(blk, msb, sq, gtg):
            postnorm_apply_with(blk, msb, stats_to_rstd(sq, blk), gtg)

        def postnorm_apply_with(blk, msb, rstd, gtg):
            cs = slice(blk * 512, (blk + 1) * 512)
            for c in range(8):
                tmp = tmp_r.next()
                tt(tmp, msb[c, cs], rstd, ALU.mult)
                stt(xT[c, cs], tmp, gtg[c:c + 1], xT[c, cs], ALU.mult, ALU.add)

        def finish_blocks_cb(msb, sq, gtg, nblk):
            def after_blk(blk, first):
                if first and nblk == 2:
                    postnorm_apply(blk, msb, sq, gtg)
                else:
                    set_tails([blk], msb, sq, gtg)
            return after_blk

        def proj_w8(wdram, idxs, rhs_fn, blks, evac_fn, after_blk, skew=2):
            idxs = list(idxs)
            blks = list(blks)
            n = len(idxs)

            def load(jp):
                slot = w8.next()
                dma_in("pool", slot, wdram[idxs[jp]].rearrange("p (a b c) -> p a b c", a=2, b=8))
                return slot

            def do(jp, slot, bl):
                for a in range(2):
                    oc = 2 * jp + a
                    for blk in bl:
                        bank = nb()
                        for ic in range(8):
                            mm(bank, slot[a, ic], rhs_fn(ic, blk), ic == 0, ic == 7)
                        evac_fn(bank, oc, blk)
                    bg_step()
            if len(blks) == 2 and skew:
                for jp in range(n - skew):
                    do(jp, load(jp), blks)
                held = {}
                for jp in range(n - skew, n):
                    held[jp] = load(jp)
                    do(jp, held[jp], [blks[0]])
                after_blk(blks[0], True)
                for jp in range(n - skew, n):
                    do(jp, held[jp], [blks[1]])
                after_blk(blks[1], False)
            else:
                for jp in range(n):
                    do(jp, load(jp), blks)
                for i, blk in enumerate(blks):
                    after_blk(blk, False)

        def ffn(l, cd, blks):
            m0 = A.mark()
            blks = list(blks)
            ncol = 512 * len(blks)
            msb = A.alloc((8, ncol), F32)
            sq = A.alloc((8, ncol), BF16)
            hT = A.alloc((8, ncol), BF16)
            actT = A.alloc((NFC, ncol), BF16)
            sg_r = tmp_r
            prenorm(blks, dvv(l, cd, 2), modT[l, cd, 24:32], hT, sq)

            def load(fc):
                slot = w8.next()
                dma_in("pool", slot, guW[l * NFC + fc].rearrange("p (a b c) -> p a b c", a=2, b=8))
                return slot

            def gu(fc, slot, bl):
                gb, ub = {}, {}
                for blk in bl:
                    gb[blk] = nb()
                    for dc in range(8):
                        mm(gb[blk], slot[0, dc], hT[dc, blk * 512:(blk + 1) * 512], dc == 0, dc == 7)
                for blk in bl:
                    ub[blk] = nb()
                    for dc in range(8):
                        mm(ub[blk], slot[1, dc], hT[dc, blk * 512:(blk + 1) * 512], dc == 0, dc == 7)
                for blk in bl:
                    sg = sg_r.next()
                    act(sg, gb[blk], AF.Silu)
                    tt(actT[fc, blk * 512:(blk + 1) * 512], sg, ub[blk], ALU.mult)

            SK = 4 if len(blks) == 2 else 0
            held = {}
            for fc in range(SK):
                held[fc] = load(fc)
                gu(fc, held[fc], [blks[0]])
            for fc in range(SK):
                gu(fc, held[fc], [blks[1]])
            for fc in range(SK, NFC):
                slot = load(fc)
                gu(fc, slot, blks)
                bg_step()
            gtg = dvv(l, cd, 3)
            after_blk = finish_blocks_cb(msb, sq, gtg, len(blks))

            def dload(dc):
                slot = wd.next()
                dma_in("pool", slot, dW[l * 8 + dc].rearrange("p (a b) -> p a b", a=NFC))
                return slot

            def down(dc, slot, bl):
                for blk in bl:
                    bank = nb()
                    for fc in range(NFC):
                        mm(bank, slot[fc], actT[fc, blk * 512:(blk + 1) * 512], fc == 0, fc == NFC - 1)
                    postnorm_evac(bank, dc, blk, gtg, msb, sq)
            if len(blks) == 2:
                for dc in range(6):
                    down(dc, dload(dc), blks)
                    bg_step()
                held = {dc: dload(dc) for dc in (6, 7)}
                for dc in (6, 7):
                    down(dc, held[dc], [blks[0]])
                after_blk(blks[0], True)
                for dc in (6, 7):
                    down(dc, held[dc], [blks[1]])
                after_blk(blks[1], False)
            else:
                for dc in range(8):
                    down(dc, dload(dc), blks)
                    bg_step()
                after_blk(blks[0], False)
            A.release(m0)

        def fourier_prep(grp, group=1, nstage=2):
            pre = {"m0": A.mark()}
            pre["AB"] = A.alloc((2, 8, 1024), BF16)
            pre["m1"] = A.mark()
            ntc = 16 if grp == "s" else 8
            xn = A.alloc((ntc, 1024), BF16)
            if grp == "s":
                pre["d0"] = A.alloc((16, 512), BF16)
                dma_in("pool", pre["d0"], dft2k_in[0].rearrange("p (a b) -> p a b", a=16))
            pre["m2"] = A.mark()
            stage_r = Ring(A, nstage, (1024,), F32)
            stt_ = [A.alloc((PG // 4,), F32) for _ in range(3)]
            pre["xn"] = xn
            memset(stt_[0][0:ntc], 0.0)

            def job(t0):
                stgs = []
                for tc in range(t0, t0 + group):
                    stg = stage_r.next()
                    stgs.append(stg)
                    dma_in("sp", stg, xtok_in[grp][:, tc * 1024:(tc + 1) * 1024])
                    act(xn[tc], stg, AF.Square, accum_out=stt_[0][tc:tc + 1])
                act(stt_[1][t0:t0 + group], stt_[0][t0:t0 + group], AF.Ln, bias=eps_v, scale=1.0 / D)
                act(stt_[2][t0:t0 + group], stt_[1][t0:t0 + group], AF.Exp, scale=-0.5)
                for i, tc in enumerate(range(t0, t0 + group)):
                    ts(xn[tc], stgs[i], stt_[2][tc:tc + 1], ALU.mult)
            for t0 in range(0, ntc, group):
                bgq.append(lambda t0=t0: job(t0))
            return pre

        def fourier(grp, cd, pre):
            m0, m1, AB, xn = pre["m0"], pre["m1"], pre["AB"], pre["xn"]
            A.release(pre["m2"])
            if grp == "s":
                d1 = A.alloc((16, 512), BF16)
                assert d1.off == pre["d1"].off
                dslots = [pre["d0"], pre["d1"]]
                for kb in range(2):
                    for t in range(2):
                        slot = dslots[(kb * 2 + t) % 2]
                        if kb * 2 + t > 1:
                            dma_in("pool", slot, dft2k_in[kb * 2 + t].rearrange("p (a b) -> p a b", a=16))
                        if kb == 0 and t == 0:
                            bks = [nb() for _ in range(8)]
                            for sc in range(16):
                                for dc in range(8):
                                    mm(bks[dc], xn[sc, dc * 128:(dc + 1) * 128], slot[sc], sc == 0, sc == 15)
                            for dc in range(8):
                                evac_copy(AB[t, dc, kb * 512:(kb + 1) * 512], bks[dc])
                            continue
                        for dc in range(8):
                            bank = nb()
                            for sc in range(16):
                                mm(bank, xn[sc, dc * 128:(dc + 1) * 128], slot[sc], sc == 0, sc == 15)
                            evac_copy(AB[t, dc, kb * 512:(kb + 1) * 512], bank)
                            bg_step()
                            if dc % 4 == 3:
                                bg_step()
            else:
                for seq in range(4):
                    for t in range(2):
                        for dc in range(8):
                            bank = nb()
                            for sc in range(2):
                                mm(bank[0:256], xn[seq * 2 + sc, dc * 128:(dc + 1) * 128], dft256[2 * t, sc], sc == 0, sc == 1)
                            evac_copy(AB[t, dc, seq * 256:(seq + 1) * 256], bank[0:256])
            A.release(m1)
            bg_flush()
            if grp == "s":
                mod_part(0, 0, "gate")
            dma_in("sp", xT, xT_in[grp].rearrange("p (a b) -> p a b", a=8))
            avec = dvv(0, cd, 0)
            bvec = modT[0, cd, 0:8]
            fT = A.alloc((8, 1024), BF16)
            sq = A.alloc((8, 1024), BF16)
            msb = A.alloc((8, 1024), F32)
            CA = A.alloc((2, 8, 256), BF16)
            bbf = A.alloc((8,), BF16)
            fdc = A.alloc((8,), F32)
            for t in range(2):
                for c in range(8):
                    ts(CA[t, c], dft256[t, c % 2], avec[c:c + 1], ALU.mult)
            copy(bbf, bvec)
            fb = nb()
            for oc in range(8):
                g, lh = oc // 2, oc % 2
                for cc in range(2):
                    mm(fb[oc:oc + 1], dft256[0, cc, lh * 128:(lh + 1) * 128], bbf[g * 2 + cc:g * 2 + cc + 1], cc == 0, cc == 1)
            copy(fdc, fb[0:8])
            sfac = scal[0:1] if grp == "s" else scal[1:2]
            dccols = {0: [0]} if grp == "s" else {0: [0, 256], 1: [0, 256]}
            for blk in range(2):
                cs = slice(blk * 512, (blk + 1) * 512)
                for oc in range(8):
                    g, lh = oc // 2, oc % 2
                    bank = nb()
                    k = 0
                    for t in range(2):
                        for cc in range(2):
                            mm(bank, CA[t, g * 2 + cc, lh * 128:(lh + 1) * 128], AB[t, g * 2 + cc, cs], k == 0, k == 3)
                            k += 1
                    evac_copy(fT[oc, cs], bank)
                    bg_step()
                    for col in dccols.get(blk, []):
                        stt(fT[oc, blk * 512 + col:blk * 512 + col + 1], fdc[oc:oc + 1], sfac, bank[col:col + 1], ALU.mult, ALU.add)
            bg_flush()
            gtg = dvv(0, cd, 1)
            proj_w8(fwW, range(4), lambda ic, blk: fT[ic, blk * 512:(blk + 1) * 512], range(2),
                    lambda bank, oc, blk: postnorm_evac(bank, oc, blk, gtg, msb, sq),
                    finish_blocks_cb(msb, sq, gtg, 2))
            A.release(m0)

        def attention(grp, cd):
            l = 1
            m0 = A.mark()
            nq = 1024 if grp == "p" else 512
            qblks = range(2) if grp == "p" else range(1)
            nch = 8 if grp == "p" else 12
            OT = A.alloc((8, nq), BF16)
            m1 = A.mark()
            O = A.alloc((nq // 128, 1024), BF16)
            Vaug = A.alloc((nch, NH, 66), BF16)
            if grp == "s":
                ckT = A.alloc((8, 512), BF16)
            rec_r = Ring(A, 3, (4,), F32)
            A.top = max(A.top, m0 + 48 * 1024)
            kT = A.alloc((8, 1024), BF16)
            qT = A.alloc((8, nq), BF16)
            m2 = A.mark()
            hT = A.alloc((8, 1024), BF16)
            sq = A.alloc((8, 512), BF16)
            stg_r = Ring(A, 3, (512,), F32) if grp == "p" else None
            if grp == "s":
                cvs = A.alloc((4, 1024), BF16)

            def late_setup():
                memset(Vaug[:, :, 64:66], 1.0)
                if grp == "s":
                    dma_in("pool", ckT, ckT_in.rearrange("p (a b) -> p a b", a=8))
                    dma_in("pool", cvs, cv_in.rearrange("p (a b) -> p a b", a=4))
                    for c in range(4):
                        src = V(cvs.ap[:, c, :].rearrange("p (h d) -> p h d", h=NH), cvs.off + c * 2048, 2, (NH, 64), (64, 1))
                        copy(Vaug[8 + c, :, 0:64], src)

            def qk_evac(bank, oc, blk):
                cs = slice(blk * 512, (blk + 1) * 512)
                if oc < 8:
                    if blk in qblks:
                        act(qT[oc, cs], bank, AF.Identity, scale=0.125)
                else:
                    hp = oc - 8
                    act(kT[hp, cs], bank, AF.Copy)
                    if grp == "p":
                        stg = stg_r.next()
                        copy(stg, bank, eng="dve")
                        dma_out("sp", kT_out[:, hp * 1024 + blk * 512: hp * 1024 + (blk + 1) * 512], stg)

            def qk_pair(jp, slot, bl):
                for a in range(2):
                    oc = 2 * jp + a
                    for blk in bl:
                        if oc < 8 and blk not in qblks:
                            continue
                        bank = nb()
                        for ic in range(8):
                            mm(bank, slot[a, ic], hT[ic, blk * 512:(blk + 1) * 512], ic == 0, ic == 7)
                        qk_evac(bank, oc, blk)

            def qk_load(jp):
                slot = w8.next()
                dma_in("pool", slot, qkW[jp].rearrange("p (a b c) -> p a b c", a=2, b=8))
                return slot

            order = [4, 5, 6, 7, 0, 1, 2, 3] if grp == "s" else list(range(8))
            held = {}
            prenorm([0], dvv(l, cd, 0), modT[l, cd, 0:8], hT, sq)
            for jp in order[:4]:
                held[jp] = qk_load(jp)
                qk_pair(jp, held[jp], [0])
            prenorm([1], dvv(l, cd, 0), modT[l, cd, 0:8], hT, sq)
            late_setup()
            for jp in order[:4]:
                qk_pair(jp, held[jp], [1])
            for jp in order[4:]:
                qk_pair(jp, qk_load(jp), [0, 1])
                bg_step()
            for vb in range(4):
                slot = w8.next()
                dma_in("pool", slot, vW[vb].rearrange("p (a b c) -> p a b c", a=2, b=8))
                sv = V(slot.ap.rearrange("p a b c -> p (a b c)").rearrange("p (d n) -> p d n", d=8), slot.off, 2, (8, 256), (256, 1))
                for tc in range(8):
                    bank = nb()
                    for dc in range(8):
                        mm(bank[0:256], hT[dc, tc * 128:(tc + 1) * 128], sv[dc], dc == 0, dc == 7)
                    bsrc = V(bank.ap[:, 0:256].rearrange("p (h d) -> p h d", h=4), 0, 4, (4, 64), (64, 1), keyname=bank.keyname)
                    evac_copy(Vaug[tc, 4 * vb:4 * vb + 4, 0:64], bsrc)
                    if grp == "p":
                        stg = stg_r.next()
                        copy(stg[0:256], bank[0:256], eng="dve")
                        dma_out("sp", v_out[:, tc * 1024 + vb * 256: tc * 1024 + (vb + 1) * 256], stg[0:256])
            A.release(m2)

            if grp == "p":
                pt_r = Ring(A, 4, (512,), BF16)

                def p_scores(seq, h):
                    hp, po = h // 2, (h % 2) * 64
                    bank = nb()
                    for kc in range(2):
                        mm(bank[kc * 256:(kc + 1) * 256],
                           kT.pp(po, po + 64)[hp, seq * 256 + kc * 128: seq * 256 + (kc + 1) * 128],
                           qT.pp(po, po + 64)[hp, seq * 256:(seq + 1) * 256], True, True)
                    pT = pt_r.next()
                    act(pT, bank, AF.Exp)
                    return pT

                def p_pv(seq, h, pT):
                    ob = nb()
                    for qb in range(2):
                        for kc in range(2):
                            mm(ob[qb * 128:qb * 128 + 65], pT[kc * 256 + qb * 128: kc * 256 + (qb + 1) * 128],
                               Vaug[seq * 2 + kc, h, 0:65], kc == 0, kc == 1)
                    rec = rec_r.next()
                    obs = V(ob.ap[:, 0:256].rearrange("p (q c) -> p q c", q=2), 0, 4, (2, 128), (128, 1), keyname=ob.keyname)
                    recip(rec[0:2], obs[:, 64])
                    for qb in range(2):
                        ts(O[seq * 2 + qb, h * 64:(h + 1) * 64], ob[qb * 128:qb * 128 + 64], rec[qb:qb + 1], ALU.mult)

                pend = []
                for seq in range(4):
                    for h in range(NH):
                        pend.append((seq, h, p_scores(seq, h)))
                        if len(pend) > 2:
                            p_pv(*pend.pop(0))
                        bg_step()
                while pend:
                    p_pv(*pend.pop(0))
            else:
                bm_r = Ring(A, 2, (8, 512), BF16)
                pt_r = Ring(A, 2, (12, 512), BF16)
                ssb_r = tmp_r

                CR = {4: (0, 128), 5: (0, 256), 6: (320, 512), 7: (448, 512)}
                for slot_ in pt_r.slots:
                    memset(slot_[4:8], 0.0)

                def s_scores(h):
                    hp, po = h // 2, (h % 2) * 64
                    bm = bm_r.next()
                    dma_in("pool", bm, BM_in[h].rearrange("p (a b) -> p a b", a=8))
                    pT = pt_r.next()
                    for c in (8, 0, 1, 9, 2, 3, 10, 4, 5, 11, 6, 7):
                        bank = nb()
                        if c < 8:
                            lo, hi = CR.get(c, (0, 512))
                            mm(bank[lo:hi], kT.pp(po, po + 64)[hp, c * 128:(c + 1) * 128], qT.pp(po, po + 64)[hp, lo:hi], True, True)
                            ssb = ssb_r.next()
                            tt(ssb[lo:hi], bank[lo:hi], bm[c, lo:hi], ALU.add)
                            act(pT[c, lo:hi], ssb[lo:hi], AF.Exp)
                        else:
                            mm(bank, ckT.pp(po, po + 64)[hp, (c - 8) * 128:(c - 7) * 128], qT.pp(po, po + 64)[hp, 0:512], True, True)
                            act(pT[c], bank, AF.Exp)
                    return pT

                def s_pv(h, pT):
                    ob = nb()
                    for qb in range(4):
                        cl = [c for c in range(12) if CR.get(c, (0, 512))[0] < (qb + 1) * 128 and CR.get(c, (0, 512))[1] > qb * 128]
                        for i, c in enumerate(cl):
                            mm(ob[qb * 128:qb * 128 + 65], pT[c, qb * 128:(qb + 1) * 128], Vaug[c, h, 0:65], i == 0, i == len(cl) - 1)
                    rec = rec_r.next()
                    obs = V(ob.ap.rearrange("p (q c) -> p q c", q=4), 0, 4, (4, 128), (128, 1), keyname=ob.keyname)
                    recip(rec, obs[:, 64])
                    for qb in range(4):
                        ts(O[qb, h * 64:(h + 1) * 64], ob[qb * 128:qb * 128 + 64], rec[qb:qb + 1], ALU.mult)

                prev = None
                for h in range(NH):
                    pT = s_scores(h)
                    if prev is not None:
                        s_pv(*prev)
                    prev = (h, pT)
                    bg_step()
                s_pv(*prev)
            for tc in range(nq // 128):
                for half in range(2):
                    bb = bf16bank(nb())
                    for j in range(4):
                        oc = half * 4 + j
                        transpose(bb[j * 128:(j + 1) * 128], O[tc, oc * 128:(oc + 1) * 128], ident)
                    bsrc = V(bb.ap[:, 0:512].rearrange("p (a b) -> p a b", a=4), 0, 2, (4, 128), (128, 1), keyname=bb.keyname)
                    evac_copy(OT[half * 4:(half + 1) * 4, tc * 128:(tc + 1) * 128], bsrc)
            A.release(m1)
            A.top = max(A.top, m0 + 64 * 1024)
            msb = A.alloc((8, nq), F32)
            sq = A.alloc((8, nq), BF16)
            gtg = dvv(l, cd, 1)
            proj_w8(owW, range(4), lambda ic, blk: OT[ic, blk * 512:(blk + 1) * 512], qblks,
                    lambda bank, oc, blk: postnorm_evac(bank, oc, blk, gtg, msb, sq),
                    finish_blocks_cb(msb, sq, gtg, len(qblks)))
            A.release(m0)

        steps = []
        pres = {}

        def start():
            pre = fourier_prep("s", group=4, nstage=8)
            pres["s"] = pre
            bg_flush()
            top = A.top
            A.top = pre["m2"]
            pre["d1"] = A.alloc((16, 512), BF16)
            A.top = top
            dma_in("pool", pre["d1"], dft2k_in[1].rearrange("p (a b) -> p a b", a=16))
            mod_part(0, 0, "ab", gate=[pre["d1"]])
            mod_part(0, 1, "ab")
        steps.append(start)
        for gi, (grp, cd) in enumerate((("s", 1), ("p", 0))):
            steps.append(lambda grp=grp, cd=cd: fourier(grp, cd, pres[grp]))

            def ffn0(gi=gi, cd=cd):
                bg_flush()
                if gi == 0:
                    mod_part(0, 1, "gate")
                    mod_part(1, 0)
                ffn(0, cd, range(2))
            steps.append(ffn0)

            def att(gi=gi, grp=grp, cd=cd):
                bg_flush()
                if gi == 0:
                    mod_part(1, 1)
                attention(grp, cd)
            steps.append(att)

            def ffn1(gi=gi, grp=grp, cd=cd):
                bg_flush()
                if gi == 0:
                    pres["p"] = fourier_prep("p")
                ffn(1, cd, range(2) if grp == "p" else range(1))
                bg_flush()
                flush_tails()
            steps.append(ffn1)

            def store_y(grp=grp):
                ncol = 1024 if grp == "p" else 512
                for blk in range(ncol // 512):
                    for c in range(8):
                        dma_out("sp", yT_out[grp][:, c * ncol + blk * 512:c * ncol + (blk + 1) * 512],
                                xT[c, blk * 512:(blk + 1) * 512])
            steps.append(store_y)
        for i, fn in enumerate(steps):
            if stage_limit is not None and i >= stage_limit:
                break
            fn()
        fin = S.add("sp", None)
        fin.deps.extend(out_events)
        S.emit()
    return nc, dbg_outs


def _fm(a):
    t = a.shape[0]
    return np.ascontiguousarray(a.T.reshape(8, 128, t).transpose(1, 0, 2).reshape(128, 8 * t))


def _tm(a):
    t = a.shape[0]
    return np.ascontiguousarray(a.reshape(t // 128, 128, 1024).transpose(1, 0, 2).reshape(128, -1))


def _w8(w):
    n = w.shape[1]
    x = w.reshape(8, 128, n // 256, 2, 128)
    return np.ascontiguousarray(x.transpose(2, 1, 3, 0, 4).reshape(n // 256, 128, 2048))


def _consts():
    s = np.arange(256)
    ang = 2.0 * np.pi * ((s[:, None] * s[None, :]) % 256) / 256.0
    c256, s256 = np.cos(ang), np.sin(ang)

    def lay(m):
        return m.reshape(2, 128, 256).transpose(1, 0, 2)
    dft256 = np.stack([lay(c256), lay(s256), lay(-s256)], axis=1).reshape(128, 1536).astype(np.float32)
    return dft256


def _core_consts(q):
    rows_w = [8 * q + i for i in range(8)] + [8 * q - 4 + i for i in range(4)] + [8 * q + 8 + i for i in range(4)]
    rows_mod = [r % 32 for r in rows_w]
    kglob = np.concatenate([np.arange(64) + 64 * r for r in rows_mod])
    s = np.arange(2048)
    prod = (s[:, None].astype(np.int64) * kglob[None, :].astype(np.int64)) % 2048
    ang = 2.0 * np.pi * prod / 2048.0
    mats = []
    for kb in range(2):
        for t in range(2):
            m = (np.cos(ang) if t == 0 else -np.sin(ang))[:, kb * 512:(kb + 1) * 512]
            mats.append(m.reshape(16, 128, 512).transpose(1, 0, 2).reshape(128, 8192))
    dft2k = np.stack(mats).astype(np.float32).astype(ml_dtypes.bfloat16)
    mask = np.full((128, 8, 512), NEG, np.float32)
    for c in range(8):
        for a in range(2):
            kr = 8 * q - 4 + WROW[c] + a
            for r in range(8):
                rg = 8 * q + r
                rs = min(max(rg - 4, 0), 24)
                if 0 <= kr < 32 and rs <= kr < rs + 8:
                    mask[a * 64:(a + 1) * 64, c, r * 64:(r + 1) * 64] = 0.0
    _cr = {4: (0, 128), 5: (0, 256), 6: (320, 512), 7: (448, 512)}
    for c, (lo, hi) in _cr.items():
        assert (mask[:, c, :lo] != 0.0).all() and (mask[:, c, hi:] != 0.0).all()
    scal = np.zeros((128, 4), np.float32)
    scal[:, 0] = 2048.0 if q == 0 else 0.0
    scal[:, 1] = 256.0
    scal[:, 2] = EPS
    return rows_mod, dft2k, mask.reshape(128, 4096), scal


def _bm_table(rpb, mask):
    a = np.arange(128) // 64
    kc = np.arange(128) % 64
    r = np.arange(512) // 64
    qc = np.arange(512) % 64
    dcol = np.clip(kc[:, None] - qc[None, :], -15, 15) + 15
    cs = np.clip(qc - 8, 0, 48)
    colok = (kc[:, None] >= cs[None, :]) & (kc[:, None] < cs[None, :] + 16)
    out = np.empty((16, 128, 8, 512), np.float32)
    for c in range(8):
        drow = np.clip((WROW[c] + a)[:, None] - (4 + r)[None, :], -7, 7) + 7
        ok = colok & (mask[:, c, :] == 0.0)
        g = rpb[:, drow, dcol]
        out[:, :, c, :] = np.where(ok[None], g, np.float32(NEG))
    return out.reshape(16, 128, 4096)


_PROG = {}


def kernel(x_prompt, x_sample, c, cache_k, cache_v, c_ctx, ada_w, ada_b, norm_g,
           fourier_w_out, na_w_qkv, na_rpb, na_w_out, ffn_w_gate, ffn_w_up, ffn_w_down, _debug=False):
    f = lambda a: np.asarray(a, dtype=np.float32)
    x_prompt, x_sample, c, cache_k, cache_v, c_ctx = map(f, (x_prompt, x_sample, c, cache_k, cache_v, c_ctx))
    ada_w, ada_b, norm_g, fourier_w_out, na_w_qkv, na_rpb, na_w_out = map(
        f, (ada_w, ada_b, norm_g, fourier_w_out, na_w_qkv, na_rpb, na_w_out))
    ffn_w_gate, ffn_w_up, ffn_w_down = map(f, (ffn_w_gate, ffn_w_up, ffn_w_down))

    key = bool(_debug)
    if key not in _PROG:
        _PROG[key] = build_program(debug=_debug)
    nc, dbg_outs = _PROG[key]

    shared = {}
    shared["adaW"] = np.concatenate([_w8(ada_w[l]) for l in range(2)], axis=0)
    adaB_l = np.ascontiguousarray(ada_b.reshape(2, 48, 128).transpose(2, 0, 1).reshape(128, 96))
    gT_l = np.ascontiguousarray(norm_g.reshape(2, 4, 8, 128).transpose(3, 0, 1, 2).reshape(128, 64))
    shared["fwW"] = _w8(fourier_w_out[0])
    shared["qkW"] = _w8(na_w_qkv[0][:, :2048])
    wv = na_w_qkv[0][:, 2048:]
    shared["vW"] = np.ascontiguousarray(wv.reshape(8, 128, 4, 256).transpose(2, 1, 0, 3).reshape(4, 128, 2048))
    shared["owW"] = _w8(na_w_out[0])
    gu = []
    for l in range(2):
        g = ffn_w_gate[l].reshape(8, 128, NFC, 128)
        u = ffn_w_up[l].reshape(8, 128, NFC, 128)
        gu.append(np.stack([g, u], axis=0).transpose(3, 2, 0, 1, 4).reshape(NFC, 128, 2048))
    shared["guW"] = np.ascontiguousarray(np.concatenate(gu, axis=0))
    dws = []
    for l in range(2):
        w = ffn_w_down[l].reshape(NFC, 128, 8, 128)
        dws.append(w.transpose(2, 1, 0, 3).reshape(8, 128, NFC * 128))
    shared["dW"] = np.ascontiguousarray(np.concatenate(dws, axis=0))
    shared["dft256"] = _consts()

    in_maps = []
    for r in range(8):
        b, q = r // 4, r % 4
        rows_mod, dft2k, mask, scal = _core_consts(q)
        m = dict(shared)
        xp = x_prompt[4 * r:4 * r + 4].reshape(1024, 1024)
        m["xT_p"] = _fm(xp)
        m["xtok_p"] = _tm(xp)
        xs = x_sample[b]
        win = xs.reshape(32, 64, 1024)[rows_mod].reshape(1024, 1024)
        m["xT_s"] = _fm(win)
        m["xtok_s"] = _tm(xs)
        cc = np.stack([c_ctx, c[b]], axis=0)
        cT_l = cc.reshape(2, 8, 128).transpose(2, 1, 0).reshape(128, 16)
        m["smalls"] = np.ascontiguousarray(np.concatenate([cT_l, adaB_l, gT_l, scal], axis=1))
        ck = cache_k[b, 0].reshape(512, 1024)
        m["ckT"] = _fm(ck)
        m["cv"] = _tm(cache_v[b, 0].reshape(512, 1024))
        m["BM"] = _bm_table(na_rpb[0], mask.reshape(128, 8, 512))
        m["dft2k"] = dft2k
        in_maps.append(m)

    res = run_bass_kernel_spmd(nc, in_maps, core_ids=list(range(8)))
    outs = res.results

    y_prompt = np.empty((32, 256, 1024), np.float32)
    y_sample = np.empty((2, 2048, 1024), np.float32)
    new_k = np.empty((32, 1, 256, NH, 64), np.float32)
    new_v = np.empty((32, 1, 256, NH, 64), np.float32)
    for r in range(8):
        b, q = r // 4, r % 4
        o = outs[r]
        yp = o["yT_p"].reshape(128, 8, 1024).transpose(2, 1, 0).reshape(1024, 1024)
        y_prompt[4 * r:4 * r + 4] = yp.reshape(4, 256, 1024)
        ys = o["yT_s"].reshape(128, 8, 512).transpose(2, 1, 0).reshape(512, 1024)
        y_sample[b, 512 * q:512 * (q + 1)] = ys
        kp = o["kT_p"].reshape(128, 8, 1024).transpose(2, 1, 0).reshape(1024, 1024)
        new_k[4 * r:4 * r + 4, 0] = kp.reshape(4, 256, NH, 64)
        vp = o["v_p"].reshape(128, 8, 1024).transpose(1, 0, 2).reshape(1024, 1024)
        new_v[4 * r:4 * r + 4, 0] = vp.reshape(4, 256, NH, 64)
    if _debug:
        return (y_prompt, y_sample, new_k, new_v), outs
    return (y_prompt, y_sample, new_k, new_v)
```

```python
import numpy as np
import ml_dtypes
from contextlib import ExitStack
import concourse.bass as bass
import concourse.mybir as mybir
from concourse.bass_utils import run_bass_kernel_spmd

F32 = mybir.dt.float32
BF16 = mybir.dt.bfloat16
AF = mybir.ActivationFunctionType
ALU = mybir.AluOpType

D = 1024
DFF = 2816
NFC = 22
NH = 16
EPS = 1e-6
NEG = -30000.0
PG = 1024
ARENA_BYTES = 207 * 1024
ENGS = ("pe", "act", "dve", "pool", "sp")
WROW = [4, 6, 8, 10, 0, 2, 12, 14]


class Op:
    __slots__ = ("eng", "fn", "deps", "dma", "marked", "ev", "reuse_wait")

    def __init__(self, eng, fn, dma):
        self.eng = eng
        self.fn = fn
        self.dma = dma
        self.deps = []
        self.marked = False
        self.ev = None
        self.reuse_wait = None


class Sched:
    def __init__(self, nc, stack, n_dma_sems=40):
        self.nc = nc
        self.ops = {e: [] for e in ENGS}
        self.last_writer = {}
        self.readers = {}
        self.esem = {e: stack.enter_context(nc.semaphore("s_" + e)) for e in ENGS}
        self.dsems = [stack.enter_context(nc.semaphore("d%d" % i)) for i in range(n_dma_sems)]
        self.duse = [0] * n_dma_sems
        self.dlast = [None] * n_dma_sems
        half = n_dma_sems * 3 // 5
        self.dpool = {"pool": list(range(0, half)), "sp": list(range(half, n_dma_sems))}
        self.dnext = {"pool": 0, "sp": 0}

    @staticmethod
    def _keys(items):
        out = []
        for k in items:
            if k is None:
                continue
            if isinstance(k, V):
                out.extend(k.keys())
            elif isinstance(k, (list, tuple)) and k and isinstance(k[0], V):
                for kk in k:
                    out.extend(kk.keys())
            else:
                out.append(k)
        return out

    def add(self, eng, fn, reads=(), writes=(), dma=False):
        op = Op(eng, fn, dma)
        reads = self._keys(reads)
        writes = self._keys(writes)
        deps = {}
        for k in reads:
            w = self.last_writer.get(k)
            if w is not None:
                deps[id(w)] = (w, True)
            if isinstance(k, str) and k.startswith("ps"):
                for r in self.readers.get(k, ()):
                    if r.eng != eng and id(r) not in deps:
                        deps[id(r)] = (r, False)
        for k in writes:
            w = self.last_writer.get(k)
            if w is not None and id(w) not in deps:
                deps[id(w)] = (w, False)
            for r in self.readers.get(k, ()):
                if id(r) not in deps:
                    deps[id(r)] = (r, False)
        for d, raw in deps.values():
            if d is op:
                continue
            if d.eng == eng and not d.dma and not dma:
                if eng == "pe":
                    continue
            op.deps.append(d)
            if not d.dma:
                d.marked = True
        for k in writes:
            self.last_writer[k] = op
            self.readers[k] = []
        for k in reads:
            self.readers.setdefault(k, []).append(op)
        if dma:
            pool = self.dpool[eng]
            i = pool[self.dnext[eng]]
            self.dnext[eng] = (self.dnext[eng] + 1) % len(pool)
            if self.dlast[i] is not None:
                op.reuse_wait = self.dlast[i]
            self.duse[i] += 1
            op.ev = (self.dsems[i], 16 * self.duse[i])
            self.dlast[i] = op
        self.ops[eng].append(op)
        return op

    def emit_engine(self, e, handle):
        seen = {}
        for op in self.ops[e]:
            waits = [d.ev for d in op.deps]
            if op.reuse_wait is not None:
                waits.append(op.reuse_wait.ev)
            for sem, val in waits:
                key = id(sem)
                if seen.get(key, 0) >= val:
                    continue
                seen[key] = val
                handle.wait_ge(sem, val)
            if op.fn is None:
                continue
            ins = op.fn(handle)
            if op.dma:
                ins.then_inc(op.ev[0], 16)
            elif op.marked:
                ins.then_inc(op.ev[0], 1)

    def emit(self):
        for e in ENGS:
            c = 0
            for op in self.ops[e]:
                if not op.dma and op.marked:
                    c += 1
                    op.ev = (self.esem[e], c)
        with self.nc.Block() as block:
            @block.tensor
            def _(h):
                self.emit_engine("pe", h)

            @block.scalar
            def _(h):
                self.emit_engine("act", h)

            @block.vector
            def _(h):
                self.emit_engine("dve", h)

            @block.gpsimd
            def _(h):
                self.emit_engine("pool", h)

            @block.sync
            def _(h):
                self.emit_engine("sp", h)


class V:
    def __init__(self, ap, off, esz, shape, strides, keyname=None):
        self.ap = ap
        self.off = off
        self.esz = esz
        self.shape = tuple(shape)
        self.strides = tuple(strides)
        self.keyname = keyname

    def keys(self):
        if self.keyname is not None:
            return [self.keyname]
        span = sum((n - 1) * s for n, s in zip(self.shape, self.strides)) + 1
        lo = self.off
        hi = self.off + span * self.esz
        return [("A", p) for p in range(lo // PG, (hi - 1) // PG + 1)]

    def __getitem__(self, idx):
        if not isinstance(idx, tuple):
            idx = (idx,)
        idx = idx + (slice(None),) * (len(self.shape) - len(idx))
        off = self.off
        shape, strides = [], []
        for i, n, s in zip(idx, self.shape, self.strides):
            if isinstance(i, slice):
                a, b, st = i.indices(n)
                cnt = len(range(a, b, st))
                off += a * s * self.esz
                shape.append(cnt)
                strides.append(s * st)
            else:
                off += i * s * self.esz
        ap = self.ap[(slice(None),) + idx]
        return V(ap, off, self.esz, shape, strides, self.keyname)

    def pp(self, lo, hi):
        return V(self.ap[lo:hi], self.off, self.esz, self.shape, self.strides, self.keyname)


class Arena:
    def __init__(self, ap_bf16, nbytes):
        self.ap = ap_bf16
        self.cap = nbytes
        self.top = 0

    def alloc(self, shape, dt, page=False):
        if isinstance(shape, int):
            shape = (shape,)
        esz = 4 if dt == F32 else 2
        n = int(np.prod(shape)) * esz
        al = PG if (n >= PG or page) else 64
        off = (self.top + al - 1) // al * al
        assert off + n <= self.cap, ("arena overflow", off, n, self.cap)
        self.top = off + n
        ap = self.ap[:, off // 2:(off + n) // 2]
        if dt == F32:
            ap = ap.bitcast(F32)
        if len(shape) == 2:
            ap = ap.rearrange("p (a b) -> p a b", a=shape[0])
        elif len(shape) == 3:
            ap = ap.rearrange("p (a b c) -> p a b c", a=shape[0], b=shape[1])
        strides = [int(np.prod(shape[i + 1:])) for i in range(len(shape))]
        return V(ap, off, esz, shape, strides)

    def mark(self):
        return self.top

    def release(self, m):
        self.top = m


class Ring:
    def __init__(self, arena, n, shape, dt):
        self.slots = [arena.alloc(shape, dt) for _ in range(n)]
        self.i = 0

    def next(self):
        s = self.slots[self.i]
        self.i = (self.i + 1) % len(self.slots)
        return s


def build_program(debug=False, stage_limit=None):
    nc = bass.Bass("TRN2", target_bir_lowering=False)

    def din(name, shape):
        return nc.dram_tensor(name, list(shape), F32, kind="ExternalInput").ap()

    def dout(name, shape):
        return nc.dram_tensor(name, list(shape), F32, kind="ExternalOutput").ap()

    xT_in = {"p": din("xT_p", (128, 8192)), "s": din("xT_s", (128, 8192))}
    xtok_in = {"p": din("xtok_p", (128, 8192)), "s": din("xtok_s", (128, 16384))}
    smalls_in = din("smalls", (128, 180))
    adaW = din("adaW", (48, 128, 2048))
    fwW = din("fwW", (4, 128, 2048))
    qkW = din("qkW", (8, 128, 2048))
    vW = din("vW", (4, 128, 2048))
    owW = din("owW", (4, 128, 2048))
    guW = din("guW", (44, 128, 2048))
    dW = din("dW", (16, 128, 2816))
    ckT_in = din("ckT", (128, 4096))
    cv_in = din("cv", (128, 4096))
    BM_in = din("BM", (16, 128, 4096))
    dft256_in = din("dft256", (128, 1536))
    dft2k_in = nc.dram_tensor("dft2k", [4, 128, 8192], BF16, kind="ExternalInput").ap()
    yT_out = {"p": dout("yT_p", (128, 8192)), "s": dout("yT_s", (128, 4096))}
    kT_out = dout("kT_p", (128, 8192))
    v_out = dout("v_p", (128, 8192))
    dbg_outs = {}

    with ExitStack() as st:
        S = Sched(nc, st)
        arena_t = st.enter_context(nc.sbuf_tensor("arena", [128, ARENA_BYTES // 2], BF16))
        A = Arena(arena_t[:, :], ARENA_BYTES)
        banks = []
        for i in range(8):
            pt = st.enter_context(nc.psum_tensor("ps%d" % i, [128, 512], F32))
            banks.append(V(pt[:, :], 0, 4, (512,), (1,), keyname="ps%d" % i))
        bstate = {"i": 0}

        def nb():
            b = banks[bstate["i"]]
            bstate["i"] = (bstate["i"] + 1) % 8
            return b

        def bf16bank(b):
            return V(b.ap.bitcast(BF16), 0, 2, (1024,), (1,), keyname=b.keyname)

        out_events = []

        def mm(out, lhsT, rhs, start, stop):
            S.add("pe", lambda e: e.matmul(out.ap, lhsT=lhsT.ap, rhs=rhs.ap, start=start, stop=stop),
                  reads=[lhsT, rhs], writes=[out])

        def transpose(out, in_, ident):
            S.add("pe", lambda e: e.transpose(out=out.ap, in_=in_.ap, identity=ident.ap),
                  reads=[in_, ident], writes=[out])

        def act(out, in_, func, bias=None, scale=None, accum_out=None):
            kw = {}
            rd = [in_]
            wr = [out]
            if bias is not None:
                kw["bias"] = bias.ap if isinstance(bias, V) else bias
                if isinstance(bias, V):
                    rd.append(bias)
            if scale is not None:
                kw["scale"] = scale.ap if isinstance(scale, V) else scale
                if isinstance(scale, V):
                    rd.append(scale)
            if accum_out is not None:
                kw["accum_out"] = accum_out.ap
                wr.append(accum_out)
            S.add("act", lambda e: e.activation(out=out.ap, in_=in_.ap, func=func, **kw), reads=rd, writes=wr)

        def tt(out, in0, in1, op, eng="dve"):
            S.add(eng, lambda e: e.tensor_tensor(out=out.ap, in0=in0.ap, in1=in1.ap, op=op),
                  reads=[in0, in1], writes=[out])

        def ts(out, in0, s1, op0, s2=None, op1=None, eng="dve"):
            rd = [in0]
            a1 = s1.ap if isinstance(s1, V) else s1
            a2 = s2.ap if isinstance(s2, V) else s2
            if isinstance(s1, V):
                rd.append(s1)
            if isinstance(s2, V):
                rd.append(s2)
            if op1 is None:
                S.add(eng, lambda e: e.tensor_scalar(out=out.ap, in0=in0.ap, scalar1=a1, scalar2=None, op0=op0),
                      reads=rd, writes=[out])
            else:
                S.add(eng, lambda e: e.tensor_scalar(out=out.ap, in0=in0.ap, scalar1=a1, scalar2=a2, op0=op0, op1=op1),
                      reads=rd, writes=[out])

        def stt(out, in0, scalar, in1, op0, op1, eng="dve"):
            rd = [in0, in1]
            sc = scalar.ap if isinstance(scalar, V) else scalar
            if isinstance(scalar, V):
                rd.append(scalar)
            S.add(eng, lambda e: e.scalar_tensor_tensor(out=out.ap, in0=in0.ap, scalar=sc, in1=in1.ap, op0=op0, op1=op1),
                  reads=rd, writes=[out])

        def copy(out, in_, eng="dve"):
            if eng == "act":
                act(out, in_, AF.Copy)
            else:
                S.add(eng, lambda e: e.tensor_copy(out=out.ap, in_=in_.ap), reads=[in_], writes=[out])

        def memset(out, val, eng="dve"):
            S.add(eng, lambda e: e.memset(out.ap, val), writes=[out])

        def recip(out, in_):
            S.add("dve", lambda e: e.reciprocal(out=out.ap, in_=in_.ap), reads=[in_], writes=[out])

        def dma_in(q, out, src, after=()):
            S.add(q, lambda e: e.dma_start(out=out.ap, in_=src), reads=list(after), writes=[out], dma=True)

        def dma_out(q, dst, in_):
            op = S.add(q, lambda e: e.dma_start(out=dst, in_=in_.ap), reads=[in_], dma=True)
            out_events.append(op)

        def dump(name, v, n):
            if not debug:
                return
            t = dout("dbg_" + name, (128, n))
            dbg_outs[name] = n
            if v.esz == 2:
                stg = A.alloc((n,), F32)
                copy(stg, v)
                dma_out("sp", t, stg)
            else:
                dma_out("sp", t, v)

        evac_rr = {"i": 0}

        def evac_copy(out, in_):
            evac_rr["i"] ^= 1
            copy(out, in_, eng="act" if evac_rr["i"] else "dve")

        ident = A.alloc((128,), BF16)
        ones = A.alloc((128,), BF16)
        dft256 = A.alloc((3, 2, 256), BF16)
        scT = A.alloc((8, 2), BF16)
        smalls = A.alloc((180,), F32, page=True)

        def sub(a, shape):
            n = int(np.prod(shape))
            ap = smalls.ap[:, a:a + n]
            if len(shape) == 2:
                ap = ap.rearrange("p (a b) -> p a b", a=shape[0])
            elif len(shape) == 3:
                ap = ap.rearrange("p (a b c) -> p a b c", a=shape[0], b=shape[1])
            strides = [int(np.prod(shape[i + 1:])) for i in range(len(shape))]
            return V(ap, smalls.off + 4 * a, 4, shape, strides)
        cT = sub(0, (8, 2))
        adaB = sub(16, (2, 48))
        gT = sub(112, (2, 4, 8))
        scal = sub(176, (4,))
        modT_pad = A.alloc((4,), F32, page=True)
        modT = A.alloc((2, 2, 48), F32)
        dv = A.alloc((2, 2, 4 * 8), F32)
        bdc = A.alloc((2, 8), F32)
        xT = A.alloc((8, 1024), F32)
        w8 = Ring(A, 5, (2, 8, 128), BF16)
        wd = Ring(A, 3, (NFC, 128), BF16)
        rstd_r = Ring(A, 2, (512,), F32)
        ln_r = Ring(A, 2, (512,), F32)
        tmp_r = Ring(A, 3, (512,), F32)
        eps_v = scal[2:3]

        memset(ident, 0.0, eng="pool")
        S.add("pool", lambda e: e.affine_select(out=ident.ap, in_=ident.ap, pattern=[[-1, 128]],
                                                compare_op=ALU.not_equal, fill=1.0, base=0, channel_multiplier=1),
              reads=[ident], writes=[ident])
        memset(ones, 1.0, eng="pool")
        dma_in("sp", smalls, smalls_in)
        dma_in("pool", dft256, dft256_in.rearrange("p (a b c) -> p a b c", a=3, b=2))
        act(scT, cT, AF.Silu)

        def dvv(l, cd, kind):
            return dv[l, cd, kind * 8:(kind + 1) * 8]

        bgq = []

        def bg_step(n=1):
            for _ in range(n):
                if bgq:
                    bgq.pop(0)()

        def bg_flush():
            while bgq:
                bgq.pop(0)()

        def mod_part(l, part, which="all", gate=()):
            def pair_job(j):
                slot = w8.next()
                dma_in("pool", slot, adaW[l * 24 + part * 12 + j].rearrange("p (a b c) -> p a b c", a=2, b=8), after=gate)
                bank = nb()
                for a in range(2):
                    for dc in range(8):
                        mm(bank[a * 2:a * 2 + 2], slot[a, dc], scT[dc], dc == 0, dc == 7)
                bv = V(bank.ap[:, 0:4].rearrange("p (o c) -> p o c", c=2), 0, 4, (2, 2), (2, 1), keyname=bank.keyname)
                o = part * 24 + 2 * j
                for cd in range(2):
                    tt(modT[l, cd, o:o + 2], bv[:, cd], adaB[l, o:o + 2], ALU.add)

            def fin_a():
                o = part * 24
                for cd in range(2):
                    stt(dvv(l, cd, 2 * part), modT[l, cd, o + 8:o + 16], 1.0, gT[l, 2 * part], ALU.add, ALU.mult)
                if l == 0 and part == 0:
                    ts(bdc[0], modT[0, 0, 0:8], scal[1:2], ALU.mult)
                    ts(bdc[1], modT[0, 1, 0:8], scal[0:1], ALU.mult)

            def fin_g():
                o = part * 24
                for cd in range(2):
                    tt(dvv(l, cd, 2 * part + 1), modT[l, cd, o + 16:o + 24], gT[l, 2 * part + 1], ALU.mult)
            if which in ("all", "ab"):
                for j in (4, 5, 6, 7, 0, 1, 2, 3):
                    bgq.append(lambda j=j: pair_job(j))
                bgq.append(fin_a)
            if which in ("all", "gate"):
                for j in (8, 9, 10, 11):
                    bgq.append(lambda j=j: pair_job(j))
                bgq.append(fin_g)

        def sqs(sq, blk):
            o = (blk * 512) % sq.shape[1]
            return slice(o, o + 512)

        def stats_to_rstd(sq, blk):
            bank = nb()
            for c in range(8):
                mm(bank, ones, sq[c, sqs(sq, blk)], c == 0, c == 7)
            ln = ln_r.next()
            rstd = rstd_r.next()
            act(ln, bank, AF.Ln, bias=eps_v, scale=1.0 / D)
            act(rstd, ln, AF.Exp, scale=-0.5)
            return rstd

        pending_tails = {}

        def flush_tails():
            for blk in sorted(pending_tails):
                pending_tails.pop(blk)()

        def set_tails(blks, msb, sq, gtg):
            for blk in blks:
                rstd = stats_to_rstd(sq, blk)
                pending_tails[blk] = (lambda blk=blk, rstd=rstd: postnorm_apply_with(blk, msb, rstd, gtg))

        def prenorm(blks, avec, bvec, hT, sq):
            for blk in blks:
                t = pending_tails.pop(blk, None)
                if t is not None:
                    t()
                cs = slice(blk * 512, (blk + 1) * 512)
                for c in range(8):
                    act(sq[c, sqs(sq, blk)], xT[c, cs], AF.Square)
                rstd = stats_to_rstd(sq, blk)
                for c in range(8):
                    tmp = tmp_r.next()
                    stt(tmp, xT[c, cs], avec[c:c + 1], rstd, ALU.mult, ALU.mult)
                    act(hT[c, cs], tmp, AF.Identity, bias=bvec[c:c + 1])

        def postnorm_evac(bank, oc, blk, gtg, msb, sq):
            cs = slice(blk * 512, (blk + 1) * 512)
            act(sq[oc, sqs(sq, blk)], bank, AF.Square)
            act(msb[oc, cs], bank, AF.Copy)

        def postnorm_apply(blk, msb, sq, gtg):
            postnorm_apply_with(blk, msb, stats_to_rstd(sq, blk), gtg)

        def postnorm_apply_with(blk, msb, rstd, gtg):
            cs = slice(blk * 512, (blk + 1) * 512)
            for c in range(8):
                tmp = tmp_r.next()
                tt(tmp, msb[c, cs], rstd, ALU.mult)
                stt(xT[c, cs], tmp, gtg[c:c + 1], xT[c, cs], ALU.mult, ALU.add)

        def finish_blocks_cb(msb, sq, gtg, nblk):
            def after_blk(blk, first):
                if first and nblk == 2:
                    postnorm_apply(blk, msb, sq, gtg)
                else:
                    set_tails([blk], msb, sq, gtg)
            return after_blk

        def proj_w8(wdram, idxs, rhs_fn, blks, evac_fn, after_blk, skew=2):
            idxs = list(idxs)
            blks = list(blks)
            n = len(idxs)

            def load(jp):
                slot = w8.next()
                dma_in("pool", slot, wdram[idxs[jp]].rearrange("p (a b c) -> p a b c", a=2, b=8))
                return slot

            def do(jp, slot, bl):
                for a in range(2):
                    oc = 2 * jp + a
                    for blk in bl:
                        bank = nb()
                        for ic in range(8):
                            mm(bank, slot[a, ic], rhs_fn(ic, blk), ic == 0, ic == 7)
                        evac_fn(bank, oc, blk)
                    bg_step()
            if len(blks) == 2 and skew:
                for jp in range(n - skew):
                    do(jp, load(jp), blks)
                held = {}
                for jp in range(n - skew, n):
                    held[jp] = load(jp)
                    do(jp, held[jp], [blks[0]])
                after_blk(blks[0], True)
                for jp in range(n - skew, n):
                    do(jp, held[jp], [blks[1]])
                after_blk(blks[1], False)
            else:
                for jp in range(n):
                    do(jp, load(jp), blks)
                for i, blk in enumerate(blks):
                    after_blk(blk, False)

        def ffn(l, cd, blks):
            m0 = A.mark()
            blks = list(blks)
            ncol = 512 * len(blks)
            msb = A.alloc((8, ncol), F32)
            sq = A.alloc((8, ncol), BF16)
            hT = A.alloc((8, ncol), BF16)
            actT = A.alloc((NFC, ncol), BF16)
            sg_r = tmp_r
            prenorm(blks, dvv(l, cd, 2), modT[l, cd, 24:32], hT, sq)

            def load(fc):
                slot = w8.next()
                dma_in("pool", slot, guW[l * NFC + fc].rearrange("p (a b c) -> p a b c", a=2, b=8))
                return slot

            def gu(fc, slot, bl):
                gb, ub = {}, {}
                for blk in bl:
                    gb[blk] = nb()
                    for dc in range(8):
                        mm(gb[blk], slot[0, dc], hT[dc, blk * 512:(blk + 1) * 512], dc == 0, dc == 7)
                for blk in bl:
                    ub[blk] = nb()
                    for dc in range(8):
                        mm(ub[blk], slot[1, dc], hT[dc, blk * 512:(blk + 1) * 512], dc == 0, dc == 7)
                for blk in bl:
                    sg = sg_r.next()
                    act(sg, gb[blk], AF.Silu)
                    tt(actT[fc, blk * 512:(blk + 1) * 512], sg, ub[blk], ALU.mult)

            SK = 4 if len(blks) == 2 else 0
            held = {}
            for fc in range(SK):
                held[fc] = load(fc)
                gu(fc, held[fc], [blks[0]])
            for fc in range(SK):
                gu(fc, held[fc], [blks[1]])
            for fc in range(SK, NFC):
                slot = load(fc)
                gu(fc, slot, blks)
                bg_step()
            gtg = dvv(l, cd, 3)
            after_blk = finish_blocks_cb(msb, sq, gtg, len(blks))

            def dload(dc):
                slot = wd.next()
                dma_in("pool", slot, dW[l * 8 + dc].rearrange("p (a b) -> p a b", a=NFC))
                return slot

            def down(dc, slot, bl):
                for blk in bl:
                    bank = nb()
                    for fc in range(NFC):
                        mm(bank, slot[fc], actT[fc, blk * 512:(blk + 1) * 512], fc == 0, fc == NFC - 1)
                    postnorm_evac(bank, dc, blk, gtg, msb, sq)
            if len(blks) == 2:
                for dc in range(6):
                    down(dc, dload(dc), blks)
                    bg_step()
                held = {dc: dload(dc) for dc in (6, 7)}
                for dc in (6, 7):
                    down(dc, held[dc], [blks[0]])
                after_blk(blks[0], True)
                for dc in (6, 7):
                    down(dc, held[dc], [blks[1]])
                after_blk(blks[1], False)
            else:
                for dc in range(8):
                    down(dc, dload(dc), blks)
                    bg_step()
                after_blk(blks[0], False)
            A.release(m0)

        def fourier_prep(grp, group=1, nstage=2):
            pre = {"m0": A.mark()}
            pre["AB"] = A.alloc((2, 8, 1024), BF16)
            pre["m1"] = A.mark()
            ntc = 16 if grp == "s" else 8
            xn = A.alloc((ntc, 1024), BF16)
            if grp == "s":
                pre["d0"] = A.alloc((16, 512), BF16)
                dma_in("pool", pre["d0"], dft2k_in[0].rearrange("p (a b) -> p a b", a=16))
            pre["m2"] = A.mark()
            stage_r = Ring(A, nstage, (1024,), F32)
            stt_ = [A.alloc((PG // 4,), F32) for _ in range(3)]
            pre["xn"] = xn
            memset(stt_[0][0:ntc], 0.0)

            def job(t0):
                stgs = []
                for tc in range(t0, t0 + group):
                    stg = stage_r.next()
                    stgs.append(stg)
                    dma_in("sp", stg, xtok_in[grp][:, tc * 1024:(tc + 1) * 1024])
                    act(xn[tc], stg, AF.Square, accum_out=stt_[0][tc:tc + 1])
                act(stt_[1][t0:t0 + group], stt_[0][t0:t0 + group], AF.Ln, bias=eps_v, scale=1.0 / D)
                act(stt_[2][t0:t0 + group], stt_[1][t0:t0 + group], AF.Exp, scale=-0.5)
                for i, tc in enumerate(range(t0, t0 + group)):
                    ts(xn[tc], stgs[i], stt_[2][tc:tc + 1], ALU.mult)
            for t0 in range(0, ntc, group):
                bgq.append(lambda t0=t0: job(t0))
            return pre

        def fourier(grp, cd, pre):
            m0, m1, AB, xn = pre["m0"], pre["m1"], pre["AB"], pre["xn"]
            A.release(pre["m2"])
            if grp == "s":
                d1 = A.alloc((16, 512), BF16)
                assert d1.off == pre["d1"].off
                dslots = [pre["d0"], pre["d1"]]
                for kb in range(2):
                    for t in range(2):
                        slot = dslots[(kb * 2 + t) % 2]
                        if kb * 2 + t > 1:
                            dma_in("pool", slot, dft2k_in[kb * 2 + t].rearrange("p (a b) -> p a b", a=16))
                        if kb == 0 and t == 0:
                            bks = [nb() for _ in range(8)]
                            for sc in range(16):
                                for dc in range(8):
                                    mm(bks[dc], xn[sc, dc * 128:(dc + 1) * 128], slot[sc], sc == 0, sc == 15)
                            for dc in range(8):
                                evac_copy(AB[t, dc, kb * 512:(kb + 1) * 512], bks[dc])
                            continue
                        for dc in range(8):
                            bank = nb()
                            for sc in range(16):
                                mm(bank, xn[sc, dc * 128:(dc + 1) * 128], slot[sc], sc == 0, sc == 15)
                            evac_copy(AB[t, dc, kb * 512:(kb + 1) * 512], bank)
                            bg_step()
                            if dc % 4 == 3:
                                bg_step()
            else:
                for seq in range(4):
                    for t in range(2):
                        for dc in range(8):
                            bank = nb()
                            for sc in range(2):
                                mm(bank[0:256], xn[seq * 2 + sc, dc * 128:(dc + 1) * 128], dft256[2 * t, sc], sc == 0, sc == 1)
                            evac_copy(AB[t, dc, seq * 256:(seq + 1) * 256], bank[0:256])
            A.release(m1)
            bg_flush()
            if grp == "s":
                mod_part(0, 0, "gate")
            dma_in("sp", xT, xT_in[grp].rearrange("p (a b) -> p a b", a=8))
            avec = dvv(0, cd, 0)
            bvec = modT[0, cd, 0:8]
            fT = A.alloc((8, 1024), BF16)
            sq = A.alloc((8, 1024), BF16)
            msb = A.alloc((8, 1024), F32)
            CA = A.alloc((2, 8, 256), BF16)
            bbf = A.alloc((8,), BF16)
            fdc = A.alloc((8,), F32)
            for t in range(2):
                for c in range(8):
                    ts(CA[t, c], dft256[t, c % 2], avec[c:c + 1], ALU.mult)
            copy(bbf, bvec)
            fb = nb()
            for oc in range(8):
                g, lh = oc // 2, oc % 2
                for cc in range(2):
                    mm(fb[oc:oc + 1], dft256[0, cc, lh * 128:(lh + 1) * 128], bbf[g * 2 + cc:g * 2 + cc + 1], cc == 0, cc == 1)
            copy(fdc, fb[0:8])
            sfac = scal[0:1] if grp == "s" else scal[1:2]
            dccols = {0: [0]} if grp == "s" else {0: [0, 256], 1: [0, 256]}
            for blk in range(2):
                cs = slice(blk * 512, (blk + 1) * 512)
                for oc in range(8):
                    g, lh = oc // 2, oc % 2
                    bank = nb()
                    k = 0
                    for t in range(2):
                        for cc in range(2):
                            mm(bank, CA[t, g * 2 + cc, lh * 128:(lh + 1) * 128], AB[t, g * 2 + cc, cs], k == 0, k == 3)
                            k += 1
                    evac_copy(fT[oc, cs], bank)
                    bg_step()
                    for col in dccols.get(blk, []):
                        stt(fT[oc, blk * 512 + col:blk * 512 + col + 1], fdc[oc:oc + 1], sfac, bank[col:col + 1], ALU.mult, ALU.add)
            bg_flush()
            gtg = dvv(0, cd, 1)
            proj_w8(fwW, range(4), lambda ic, blk: fT[ic, blk * 512:(blk + 1) * 512], range(2),
                    lambda bank, oc, blk: postnorm_evac(bank, oc, blk, gtg, msb, sq),
                    finish_blocks_cb(msb, sq, gtg, 2))
            A.release(m0)

        def attention(grp, cd):
            l = 1
            m0 = A.mark()
            nq = 1024 if grp == "p" else 512
            qblks = range(2) if grp == "p" else range(1)
            nch = 8 if grp == "p" else 12
            if grp == "p":
                OT = A.alloc((8, nq), BF16)
                m1 = A.mark()
                O = A.alloc((nq // 128, 1024), BF16)
            else:
                O = A.alloc((nq // 128, 1024), BF16)
                m1 = A.mark()
            Vaug = A.alloc((nch, NH, 66), BF16)
            rec_r = Ring(A, 3, (4,), F32)
            if grp == "s":
                ck_r = Ring(A, 2, (512,), BF16)
                bm_r = Ring(A, 2, (8, 512), BF16)
            A.top = max(A.top, m0 + 48 * 1024)
            kT = A.alloc((8, 1024), BF16)
            if grp == "p":
                qT = A.alloc((8, nq), BF16)
            else:
                qz = A.alloc((NH, 512), BF16)
            m2 = A.mark()
            hT = A.alloc((8, 1024), BF16)
            sq = A.alloc((8, 512), BF16)
            stg_r = Ring(A, 3, (512,), F32) if grp == "p" else None
            if grp == "s":
                cvs = A.alloc((4, 1024), BF16)

            def late_setup():
                memset(Vaug[:, :, 64:66], 1.0)
                if grp == "s":
                    memset(qz[0::2].pp(64, 128), 0.0)
                    memset(qz[1::2].pp(0, 64), 0.0)
                    dma_in("pool", cvs, cv_in.rearrange("p (a b) -> p a b", a=4))
                    for c in range(4):
                        src = V(cvs.ap[:, c, :].rearrange("p (h d) -> p h d", h=NH), cvs.off + c * 2048, 2, (NH, 64), (64, 1))
                        copy(Vaug[8 + c, :, 0:64], src)

            def qk_evac(bank, oc, blk):
                cs = slice(blk * 512, (blk + 1) * 512)
                if oc < 8:
                    if blk in qblks and grp == "p":
                        act(qT[oc, cs], bank, AF.Identity, scale=0.125)
                    elif blk in qblks:
                        act(qz[2 * oc].pp(0, 64), bank.pp(0, 64), AF.Identity, scale=0.125)
                        act(qz[2 * oc + 1].pp(64, 128), bank.pp(64, 128), AF.Identity, scale=0.125)
                else:
                    hp = oc - 8
                    act(kT[hp, cs], bank, AF.Copy)
                    if grp == "p":
                        stg = stg_r.next()
                        copy(stg, bank, eng="dve")
                        dma_out("sp", kT_out[:, hp * 1024 + blk * 512: hp * 1024 + (blk + 1) * 512], stg)

            def qk_pair(jp, slot, bl):
                for a in range(2):
                    oc = 2 * jp + a
                    for blk in bl:
                        if oc < 8 and blk not in qblks:
                            continue
                        bank = nb()
                        for ic in range(8):
                            mm(bank, slot[a, ic], hT[ic, blk * 512:(blk + 1) * 512], ic == 0, ic == 7)
                        qk_evac(bank, oc, blk)

            def qk_load(jp):
                slot = w8.next()
                dma_in("pool", slot, qkW[jp].rearrange("p (a b c) -> p a b c", a=2, b=8))
                return slot

            order = [4, 5, 6, 7, 0, 1, 2, 3] if grp == "s" else list(range(8))
            held = {}
            prenorm([0], dvv(l, cd, 0), modT[l, cd, 0:8], hT, sq)
            for jp in order[:4]:
                held[jp] = qk_load(jp)
                qk_pair(jp, held[jp], [0])
            prenorm([1], dvv(l, cd, 0), modT[l, cd, 0:8], hT, sq)
            late_setup()
            for jp in order[:4]:
                qk_pair(jp, held[jp], [1])
            for jp in order[4:]:
                qk_pair(jp, qk_load(jp), [0, 1])
                bg_step()
            for vb in range(4):
                slot = w8.next()
                dma_in("pool", slot, vW[vb].rearrange("p (a b c) -> p a b c", a=2, b=8))
                sv = V(slot.ap.rearrange("p a b c -> p (a b c)").rearrange("p (d n) -> p d n", d=8), slot.off, 2, (8, 256), (256, 1))
                for tc in range(8):
                    bank = nb()
                    for dc in range(8):
                        mm(bank[0:256], hT[dc, tc * 128:(tc + 1) * 128], sv[dc], dc == 0, dc == 7)
                    bsrc = V(bank.ap[:, 0:256].rearrange("p (h d) -> p h d", h=4), 0, 4, (4, 64), (64, 1), keyname=bank.keyname)
                    evac_copy(Vaug[tc, 4 * vb:4 * vb + 4, 0:64], bsrc)
                    if grp == "p":
                        stg = stg_r.next()
                        copy(stg[0:256], bank[0:256], eng="dve")
                        dma_out("sp", v_out[:, tc * 1024 + vb * 256: tc * 1024 + (vb + 1) * 256], stg[0:256])
            A.release(m2)

            if grp == "p":
                pt_r = Ring(A, 4, (512,), BF16)

                def p_scores(seq, h):
                    hp, po = h // 2, (h % 2) * 64
                    bank = nb()
                    for kc in range(2):
                        mm(bank[kc * 256:(kc + 1) * 256],
                           kT.pp(po, po + 64)[hp, seq * 256 + kc * 128: seq * 256 + (kc + 1) * 128],
                           qT.pp(po, po + 64)[hp, seq * 256:(seq + 1) * 256], True, True)
                    pT = pt_r.next()
                    act(pT, bank, AF.Exp)
                    return pT

                def p_pv(seq, h, pT):
                    ob = nb()
                    for qb in range(2):
                        for kc in range(2):
                            mm(ob[qb * 128:qb * 128 + 65], pT[kc * 256 + qb * 128: kc * 256 + (qb + 1) * 128],
                               Vaug[seq * 2 + kc, h, 0:65], kc == 0, kc == 1)
                    rec = rec_r.next()
                    obs = V(ob.ap[:, 0:256].rearrange("p (q c) -> p q c", q=2), 0, 4, (2, 128), (128, 1), keyname=ob.keyname)
                    recip(rec[0:2], obs[:, 64])
                    for qb in range(2):
                        ts(O[seq * 2 + qb, h * 64:(h + 1) * 64], ob[qb * 128:qb * 128 + 64], rec[qb:qb + 1], ALU.mult)

                pend = []
                for seq in range(4):
                    for h in range(NH):
                        pend.append((seq, h, p_scores(seq, h)))
                        if len(pend) > 2:
                            p_pv(*pend.pop(0))
                        bg_step()
                while pend:
                    p_pv(*pend.pop(0))
            else:
                pt_r = Ring(A, 2, (12, 512), BF16)
                ssb_r = tmp_r
                ckh = {}

                CR = {4: (0, 128), 5: (0, 256), 6: (320, 512), 7: (448, 512)}
                for slot_ in pt_r.slots:
                    memset(slot_[4:8], 0.0)

                def s_scores(h):
                    hp, po = h // 2, (h % 2) * 64
                    bm = bm_r.next()
                    dma_in("pool", bm, BM_in[h].rearrange("p (a b) -> p a b", a=8))
                    pT = pt_r.next()
                    if h % 2 == 0:
                        ckh["v"] = ck_r.next()
                        dma_in("pool", ckh["v"], ckT_in[:, hp * 512:(hp + 1) * 512])
                    ck = ckh["v"]
                    for c in (8, 0, 1, 9, 2, 3, 10, 4, 5, 11, 6, 7):
                        bank = nb()
                        if c < 8:
                            lo, hi = CR.get(c, (0, 512))
                            mm(bank[lo:hi], kT[hp, c * 128:(c + 1) * 128], qz[h, lo:hi], True, True)
                            ssb = ssb_r.next()
                            tt(ssb[lo:hi], bank[lo:hi], bm[c, lo:hi], ALU.add)
                            act(pT[c, lo:hi], ssb[lo:hi], AF.Exp)
                        else:
                            mm(bank, ck[(c - 8) * 128:(c - 7) * 128], qz[h], True, True)
                            act(pT[c], bank, AF.Exp)
                    return pT

                def s_pv(h, pT):
                    ob = nb()
                    for qb in range(4):
                        cl = [c for c in range(12) if CR.get(c, (0, 512))[0] < (qb + 1) * 128 and CR.get(c, (0, 512))[1] > qb * 128]
                        for i, c in enumerate(cl):
                            mm(ob[qb * 128:qb * 128 + 65], pT[c, qb * 128:(qb + 1) * 128], Vaug[c, h, 0:65], i == 0, i == len(cl) - 1)
                    rec = rec_r.next()
                    obs = V(ob.ap.rearrange("p (q c) -> p q c", q=4), 0, 4, (4, 128), (128, 1), keyname=ob.keyname)
                    recip(rec, obs[:, 64])
                    for qb in range(4):
                        ts(O[qb, h * 64:(h + 1) * 64], ob[qb * 128:qb * 128 + 64], rec[qb:qb + 1], ALU.mult)

                prev = None
                for h in range(NH):
                    pT = s_scores(h)
                    if prev is not None:
                        s_pv(*prev)
                    prev = (h, pT)
                    bg_step()
                s_pv(*prev)
            if grp == "s":
                A.release(m1)
                OT = A.alloc((8, nq), BF16)
            for tc in range(nq // 128):
                for half in range(2):
                    bb = bf16bank(nb())
                    for j in range(4):
                        oc = half * 4 + j
                        transpose(bb[j * 128:(j + 1) * 128], O[tc, oc * 128:(oc + 1) * 128], ident)
                    bsrc = V(bb.ap[:, 0:512].rearrange("p (a b) -> p a b", a=4), 0, 2, (4, 128), (128, 1), keyname=bb.keyname)
                    evac_copy(OT[half * 4:(half + 1) * 4, tc * 128:(tc + 1) * 128], bsrc)
            if grp == "p":
                A.release(m1)
            A.top = max(A.top, m0 + 64 * 1024)
            msb = A.alloc((8, nq), F32)
            sq = A.alloc((8, nq), BF16)
            gtg = dvv(l, cd, 1)
            proj_w8(owW, range(4), lambda ic, blk: OT[ic, blk * 512:(blk + 1) * 512], qblks,
                    lambda bank, oc, blk: postnorm_evac(bank, oc, blk, gtg, msb, sq),
                    finish_blocks_cb(msb, sq, gtg, len(qblks)))
            A.release(m0)

        steps = []
        pres = {}

        def start():
            pre = fourier_prep("s", group=4, nstage=8)
            pres["s"] = pre
            bg_flush()
            top = A.top
            A.top = pre["m2"]
            pre["d1"] = A.alloc((16, 512), BF16)
            A.top = top
            dma_in("pool", pre["d1"], dft2k_in[1].rearrange("p (a b) -> p a b", a=16))
            mod_part(0, 0, "ab", gate=[pre["d1"]])
            mod_part(0, 1, "ab")
        steps.append(start)
        for gi, (grp, cd) in enumerate((("s", 1), ("p", 0))):
            steps.append(lambda grp=grp, cd=cd: fourier(grp, cd, pres[grp]))

            def ffn0(gi=gi, cd=cd):
                bg_flush()
                if gi == 0:
                    mod_part(0, 1, "gate")
                    mod_part(1, 0)
                ffn(0, cd, range(2))
            steps.append(ffn0)

            def att(gi=gi, grp=grp, cd=cd):
                bg_flush()
                if gi == 0:
                    mod_part(1, 1)
                attention(grp, cd)
            steps.append(att)

            def ffn1(gi=gi, grp=grp, cd=cd):
                bg_flush()
                if gi == 0:
                    pres["p"] = fourier_prep("p")
                ffn(1, cd, range(2) if grp == "p" else range(1))
                bg_flush()
                flush_tails()
            steps.append(ffn1)

            def store_y(grp=grp):
                ncol = 1024 if grp == "p" else 512
                for blk in range(ncol // 512):
                    for c in range(8):
                        dma_out("sp", yT_out[grp][:, c * ncol + blk * 512:c * ncol + (blk + 1) * 512],
                                xT[c, blk * 512:(blk + 1) * 512])
            steps.append(store_y)
        for i, fn in enumerate(steps):
            if stage_limit is not None and i >= stage_limit:
                break
            fn()
        fin = S.add("sp", None)
        fin.deps.extend(out_events)
        S.emit()
    return nc, dbg_outs


def _fm(a):
    t = a.shape[0]
    return np.ascontiguousarray(a.T.reshape(8, 128, t).transpose(1, 0, 2).reshape(128, 8 * t))


def _tm(a):
    t = a.shape[0]
    return np.ascontiguousarray(a.reshape(t // 128, 128, 1024).transpose(1, 0, 2).reshape(128, -1))


def _w8(w):
    n = w.shape[1]
    x = w.reshape(8, 128, n // 256, 2, 128)
    return np.ascontiguousarray(x.transpose(2, 1, 3, 0, 4).reshape(n // 256, 128, 2048))


def _consts():
    s = np.arange(256)
    ang = 2.0 * np.pi * ((s[:, None] * s[None, :]) % 256) / 256.0
    c256, s256 = np.cos(ang), np.sin(ang)

    def lay(m):
        return m.reshape(2, 128, 256).transpose(1, 0, 2)
    dft256 = np.stack([lay(c256), lay(s256), lay(-s256)], axis=1).reshape(128, 1536).astype(np.float32)
    return dft256


def _core_consts(q):
    rows_w = [8 * q + i for i in range(8)] + [8 * q - 4 + i for i in range(4)] + [8 * q + 8 + i for i in range(4)]
    rows_mod = [r % 32 for r in rows_w]
    kglob = np.concatenate([np.arange(64) + 64 * r for r in rows_mod])
    s = np.arange(2048)
    prod = (s[:, None].astype(np.int64) * kglob[None, :].astype(np.int64)) % 2048
    ang = 2.0 * np.pi * prod / 2048.0
    mats = []
    for kb in range(2):
        for t in range(2):
            m = (np.cos(ang) if t == 0 else -np.sin(ang))[:, kb * 512:(kb + 1) * 512]
            mats.append(m.reshape(16, 128, 512).transpose(1, 0, 2).reshape(128, 8192))
    dft2k = np.stack(mats).astype(np.float32).astype(ml_dtypes.bfloat16)
    mask = np.full((128, 8, 512), NEG, np.float32)
    for c in range(8):
        for a in range(2):
            kr = 8 * q - 4 + WROW[c] + a
            for r in range(8):
                rg = 8 * q + r
                rs = min(max(rg - 4, 0), 24)
                if 0 <= kr < 32 and rs <= kr < rs + 8:
                    mask[a * 64:(a + 1) * 64, c, r * 64:(r + 1) * 64] = 0.0
    _cr = {4: (0, 128), 5: (0, 256), 6: (320, 512), 7: (448, 512)}
    for c, (lo, hi) in _cr.items():
        assert (mask[:, c, :lo] != 0.0).all() and (mask[:, c, hi:] != 0.0).all()
    scal = np.zeros((128, 4), np.float32)
    scal[:, 0] = 2048.0 if q == 0 else 0.0
    scal[:, 1] = 256.0
    scal[:, 2] = EPS
    return rows_mod, dft2k, mask.reshape(128, 4096), scal


def _bm_table(rpb, mask):
    a = np.arange(128) // 64
    kc = np.arange(128) % 64
    r = np.arange(512) // 64
    qc = np.arange(512) % 64
    dcol = np.clip(kc[:, None] - qc[None, :], -15, 15) + 15
    cs = np.clip(qc - 8, 0, 48)
    colok = (kc[:, None] >= cs[None, :]) & (kc[:, None] < cs[None, :] + 16)
    out = np.empty((16, 128, 8, 512), np.float32)
    for c in range(8):
        drow = np.clip((WROW[c] + a)[:, None] - (4 + r)[None, :], -7, 7) + 7
        ok = colok & (mask[:, c, :] == 0.0)
        g = rpb[:, drow, dcol]
        out[:, :, c, :] = np.where(ok[None], g, np.float32(NEG))
    return out.reshape(16, 128, 4096)


_PROG = {}


def kernel(x_prompt, x_sample, c, cache_k, cache_v, c_ctx, ada_w, ada_b, norm_g,
           fourier_w_out, na_w_qkv, na_rpb, na_w_out, ffn_w_gate, ffn_w_up, ffn_w_down, _debug=False):
    f = lambda a: np.asarray(a, dtype=np.float32)
    x_prompt, x_sample, c, cache_k, cache_v, c_ctx = map(f, (x_prompt, x_sample, c, cache_k, cache_v, c_ctx))
    ada_w, ada_b, norm_g, fourier_w_out, na_w_qkv, na_rpb, na_w_out = map(
        f, (ada_w, ada_b, norm_g, fourier_w_out, na_w_qkv, na_rpb, na_w_out))
    ffn_w_gate, ffn_w_up, ffn_w_down = map(f, (ffn_w_gate, ffn_w_up, ffn_w_down))

    key = bool(_debug)
    if key not in _PROG:
        _PROG[key] = build_program(debug=_debug)
    nc, dbg_outs = _PROG[key]

    shared = {}
    shared["adaW"] = np.concatenate([_w8(ada_w[l]) for l in range(2)], axis=0)
    adaB_l = np.ascontiguousarray(ada_b.reshape(2, 48, 128).transpose(2, 0, 1).reshape(128, 96))
    gT_l = np.ascontiguousarray(norm_g.reshape(2, 4, 8, 128).transpose(3, 0, 1, 2).reshape(128, 64))
    shared["fwW"] = _w8(fourier_w_out[0])
    shared["qkW"] = _w8(na_w_qkv[0][:, :2048])
    wv = na_w_qkv[0][:, 2048:]
    shared["vW"] = np.ascontiguousarray(wv.reshape(8, 128, 4, 256).transpose(2, 1, 0, 3).reshape(4, 128, 2048))
    shared["owW"] = _w8(na_w_out[0])
    gu = []
    for l in range(2):
        g = ffn_w_gate[l].reshape(8, 128, NFC, 128)
        u = ffn_w_up[l].reshape(8, 128, NFC, 128)
        gu.append(np.stack([g, u], axis=0).transpose(3, 2, 0, 1, 4).reshape(NFC, 128, 2048))
    shared["guW"] = np.ascontiguousarray(np.concatenate(gu, axis=0))
    dws = []
    for l in range(2):
        w = ffn_w_down[l].reshape(NFC, 128, 8, 128)
        dws.append(w.transpose(2, 1, 0, 3).reshape(8, 128, NFC * 128))
    shared["dW"] = np.ascontiguousarray(np.concatenate(dws, axis=0))
    shared["dft256"] = _consts()

    in_maps = []
    for r in range(8):
        b, q = r // 4, r % 4
        rows_mod, dft2k, mask, scal = _core_consts(q)
        m = dict(shared)
        xp = x_prompt[4 * r:4 * r + 4].reshape(1024, 1024)
        m["xT_p"] = _fm(xp)
        m["xtok_p"] = _tm(xp)
        xs = x_sample[b]
        win = xs.reshape(32, 64, 1024)[rows_mod].reshape(1024, 1024)
        m["xT_s"] = _fm(win)
        m["xtok_s"] = _tm(xs)
        cc = np.stack([c_ctx, c[b]], axis=0)
        cT_l = cc.reshape(2, 8, 128).transpose(2, 1, 0).reshape(128, 16)
        m["smalls"] = np.ascontiguousarray(np.concatenate([cT_l, adaB_l, gT_l, scal], axis=1))
        ck = cache_k[b, 0].reshape(512, 1024)
        m["ckT"] = _fm(ck)
        m["cv"] = _tm(cache_v[b, 0].reshape(512, 1024))
        m["BM"] = _bm_table(na_rpb[0], mask.reshape(128, 8, 512))
        m["dft2k"] = dft2k
        in_maps.append(m)

    res = run_bass_kernel_spmd(nc, in_maps, core_ids=list(range(8)))
    outs = res.results

    y_prompt = np.empty((32, 256, 1024), np.float32)
    y_sample = np.empty((2, 2048, 1024), np.float32)
    new_k = np.empty((32, 1, 256, NH, 64), np.float32)
    new_v = np.empty((32, 1, 256, NH, 64), np.float32)
    for r in range(8):
        b, q = r // 4, r % 4
        o = outs[r]
        yp = o["yT_p"].reshape(128, 8, 1024).transpose(2, 1, 0).reshape(1024, 1024)
        y_prompt[4 * r:4 * r + 4] = yp.reshape(4, 256, 1024)
        ys = o["yT_s"].reshape(128, 8, 512).transpose(2, 1, 0).reshape(512, 1024)
        y_sample[b, 512 * q:512 * (q + 1)] = ys
        kp = o["kT_p"].reshape(128, 8, 1024).transpose(2, 1, 0).reshape(1024, 1024)
        new_k[4 * r:4 * r + 4, 0] = kp.reshape(4, 256, NH, 64)
        vp = o["v_p"].reshape(128, 8, 1024).transpose(1, 0, 2).reshape(1024, 1024)
        new_v[4 * r:4 * r + 4, 0] = vp.reshape(4, 256, NH, 64)
    if _debug:
        return (y_prompt, y_sample, new_k, new_v), outs
    return (y_prompt, y_sample, new_k, new_v)
```
